# Optimizing a Trainium2 kernel written in Bass

```python
import jax, jax.numpy as jnp
from jax import lax
import numpy as np

D_MODEL = 1024
BATCH = 8
SEQ = 8192
DEPTH = 4
DEC_BATCH = 16
DEC_SEQ = 2048
PAST_LEN = 128

N_MEM = 256
POOL_WINDOWS = (2, 4, 8, 16)
POOL_GROUPS = len(POOL_WINDOWS)
POOL_WIDTH = D_MODEL // 2
POOL_GROUP_W = POOL_WIDTH // POOL_GROUPS
MLA_HEADS = 8
QK_NOPE = 64
QK_ROPE = 32
V_HEAD = 64
Q_LORA = 384
KV_LORA = 256
MLA_WIDTH = MLA_HEADS * V_HEAD
ROPE_THETA = 10000.0
Q_BLOCK = 128
X_HEADS = 4
X_HEAD_DIM = 128
X_WIDTH = X_HEADS * X_HEAD_DIM
N_BRANCH = 3
BRANCH_WIDTH = 512
IN_SPLITS = (POOL_WIDTH, Q_LORA, KV_LORA, QK_ROPE, X_WIDTH)
IN_WIDTH = sum(IN_SPLITS)
IN_OFFSETS = tuple(int(v) for v in np.cumsum(IN_SPLITS)[:-1])
D_FF = -(-8 * D_MODEL // (3 * 256)) * 256
EPS = 1e-6

kernel_name = 'hybrid_pool_mla_memory_encoder'


def rmsnorm(x, g):
    xf = x.astype(jnp.float32)
    y = xf * lax.rsqrt(jnp.mean(xf * xf, axis=-1, keepdims=True) + EPS)
    return (y * g.astype(jnp.float32)).astype(x.dtype)


def rope_tables(seq, dtype):
    inv = 1.0 / (ROPE_THETA ** (np.arange(0, QK_ROPE, 2, dtype=np.float32) / QK_ROPE))
    ang = np.arange(seq, dtype=np.float32)[:, None] * inv[None, :]
    return jnp.asarray(np.cos(ang), dtype), jnp.asarray(np.sin(ang), dtype)


def apply_rope(x, cos, sin):
    half = x.shape[-1] // 2
    x1, x2 = x[..., :half], x[..., half:]
    return jnp.concatenate([x1 * cos - x2 * sin, x2 * cos + x1 * sin], axis=-1)


def multiscale_pool(u, mix, scale):
    B, S, _ = u.shape
    ug = u.reshape(B, S, POOL_GROUPS, POOL_GROUP_W)
    uf = ug.astype(jnp.float32)
    c = jnp.pad(jnp.cumsum(uf, axis=1), ((0, 0), (1, 0), (0, 0), (0, 0)))
    t = np.arange(S)
    pooled = []
    for gi, w in enumerate(POOL_WINDOWS):
        lo = np.clip(t - w // 2, 0, S)
        hi = np.clip(t + w // 2, 0, S)
        cnt = (hi - lo).astype(np.float32)[None, :, None]
        cg = c[:, :, gi]
        pooled.append((cg[:, hi] - cg[:, lo]) / cnt)
    pooled = jnp.stack(pooled, axis=2)
    diff = (pooled - uf).astype(u.dtype)
    y = jnp.einsum('bsgc,gcd->bsgd', diff, mix).reshape(B, S, POOL_WIDTH)
    return y * scale


def latent_attention(cq, ckv, kr, q_norm, kv_norm, w_uq, w_uk, w_uv, cos, sin):
    B, S, _ = cq.shape
    q = (rmsnorm(cq, q_norm) @ w_uq).reshape(B, S, MLA_HEADS, QK_NOPE + QK_ROPE)
    qn = q[..., :QK_NOPE]
    qr = apply_rope(q[..., QK_NOPE:], cos[:, None, :], sin[:, None, :])
    c = rmsnorm(ckv, kv_norm)
    kn = (c @ w_uk).reshape(B, S, MLA_HEADS, QK_NOPE)
    v = (c @ w_uv).reshape(B, S, MLA_HEADS, V_HEAD)
    kr = apply_rope(kr, cos, sin)
    scale = (QK_NOPE + QK_ROPE) ** -0.5
    nb = S // Q_BLOCK
    qn_b = qn.reshape(B, nb, Q_BLOCK, MLA_HEADS, QK_NOPE).transpose(1, 0, 2, 3, 4)
    qr_b = qr.reshape(B, nb, Q_BLOCK, MLA_HEADS, QK_ROPE).transpose(1, 0, 2, 3, 4)

    def block(args):
        qnb, qrb = args
        s = jnp.einsum('bqhd,bkhd->bhqk', qnb, kn) + jnp.einsum('bqhr,bkr->bhqk', qrb, kr)
        p = jax.nn.softmax(s.astype(jnp.float32) * scale, axis=-1).astype(v.dtype)
        return jnp.einsum('bhqk,bkhd->bqhd', p, v)

    o = lax.map(block, (qn_b, qr_b))
    return o.transpose(1, 0, 2, 3, 4).reshape(B, S, MLA_WIDTH)


def memory_attention(qx, mem_n, w_mem_kv):
    B, S, _ = qx.shape
    q = qx.reshape(B, S, X_HEADS, X_HEAD_DIM)
    kv = mem_n @ w_mem_kv
    k = kv[..., :X_WIDTH].reshape(B, -1, X_HEADS, X_HEAD_DIM)
    v = kv[..., X_WIDTH:].reshape(B, -1, X_HEADS, X_HEAD_DIM)
    s = jnp.einsum('bshd,bmhd->bhsm', q, k)
    p = jax.nn.softmax(s.astype(jnp.float32) * (X_HEAD_DIM ** -0.5), axis=-1).astype(v.dtype)
    return jnp.einsum('bhsm,bmhd->bshd', p, v).reshape(B, S, X_WIDTH)


def encoder_layer(x, mem, cos, sin, w_in, q_norm, kv_norm, w_uq, w_uk, w_uv, pool_mix, pool_scale,
                  mem_norm, w_mem_kv, w_branch, w_gate, b_gate, w_out,
                  ln_mix_pre, ln_mix_post, ln_ffn_pre, ln_ffn_post, w_gu, w_down):
    B, S, D = x.shape
    h = rmsnorm(x, ln_mix_pre)
    z = h @ w_in
    u_pool, cq, ckv, kr, qx = jnp.split(z, IN_OFFSETS, axis=-1)
    a_out = multiscale_pool(u_pool, pool_mix, pool_scale)
    b_out = latent_attention(cq, ckv, kr, q_norm, kv_norm, w_uq, w_uk, w_uv, cos, sin)
    m_out = memory_attention(qx, rmsnorm(mem, mem_norm), w_mem_kv)
    br = jnp.stack([a_out, b_out, m_out], axis=2)
    br = jnp.einsum('bsnc,ncd->bsnd', br, w_branch)
    g = jax.nn.sigmoid(h @ w_gate + b_gate).reshape(B, S, N_BRANCH, D)
    merged = jnp.sum(g * br, axis=2)
    x = x + rmsnorm(merged @ w_out, ln_mix_post)
    h = rmsnorm(x, ln_ffn_pre)
    gu = h @ w_gu
    f = (jax.nn.silu(gu[..., :D_FF]) * gu[..., D_FF:]) @ w_down
    return x + rmsnorm(f, ln_ffn_post)


def setup_inputs(seed: int = 0) -> dict:
    key = jax.random.key(seed)
    ks = jax.random.split(key, 26)

    def nrm(k, shape, scale):
        return jax.random.normal(k, shape, jnp.float32) * scale

    def gain(k, shape):
        return 1.0 + 0.02 * jax.random.normal(k, shape, jnp.float32)

    L, D = DEPTH, D_MODEL
    return {
        'x_prompt': nrm(ks[0], (BATCH, SEQ, D), 1.0),
        'x_sample': nrm(ks[1], (DEC_BATCH, DEC_SEQ, D), 1.0),
        'mem_prompt': nrm(ks[2], (BATCH, N_MEM, D), 1.0),
        'mem_sample': nrm(ks[3], (DEC_BATCH, N_MEM, D), 1.0),
        'w_in': nrm(ks[4], (L, D, IN_WIDTH), D ** -0.5),
        'q_norm': gain(ks[5], (L, Q_LORA)),
        'kv_norm': gain(ks[6], (L, KV_LORA)),
        'w_uq': nrm(ks[7], (L, Q_LORA, MLA_HEADS * (QK_NOPE + QK_ROPE)), Q_LORA ** -0.5),
        'w_uk': nrm(ks[8], (L, KV_LORA, MLA_HEADS * QK_NOPE), KV_LORA ** -0.5),
        'w_uv': nrm(ks[9], (L, KV_LORA, MLA_HEADS * V_HEAD), KV_LORA ** -0.5),
        'pool_mix': nrm(ks[10], (L, POOL_GROUPS, POOL_GROUP_W, POOL_GROUP_W), POOL_GROUP_W ** -0.5),
        'pool_scale': gain(ks[11], (L, POOL_WIDTH)),
        'mem_norm': gain(ks[12], (L, D)),
        'w_mem_kv': nrm(ks[13], (L, D, 2 * X_WIDTH), D ** -0.5),
        'w_branch': nrm(ks[14], (L, N_BRANCH, BRANCH_WIDTH, D), BRANCH_WIDTH ** -0.5),
        'w_gate': nrm(ks[15], (L, D, N_BRANCH * D), D ** -0.5),
        'b_gate': nrm(ks[16], (L, N_BRANCH * D), 0.02),
        'w_out': nrm(ks[17], (L, D, D), D ** -0.5),
        'ln_mix_pre': gain(ks[18], (L, D)),
        'ln_mix_post': gain(ks[19], (L, D)),
        'ln_ffn_pre': gain(ks[20], (L, D)),
        'ln_ffn_post': gain(ks[21], (L, D)),
        'w_gu': nrm(ks[22], (L, D, 2 * D_FF), D ** -0.5),
        'w_down': nrm(ks[23], (L, D_FF, D), D_FF ** -0.5),
    }


def reference(x_prompt, x_sample, mem_prompt, mem_sample, w_in, q_norm, kv_norm, w_uq, w_uk, w_uv,
              pool_mix, pool_scale, mem_norm, w_mem_kv, w_branch, w_gate, b_gate, w_out,
              ln_mix_pre, ln_mix_post, ln_ffn_pre, ln_ffn_post, w_gu, w_down):
    def trunk(x, mem):
        cos, sin = rope_tables(x.shape[1], x.dtype)
        for l in range(DEPTH):
            x = encoder_layer(x, mem, cos, sin, w_in[l], q_norm[l], kv_norm[l], w_uq[l], w_uk[l], w_uv[l],
                              pool_mix[l], pool_scale[l], mem_norm[l], w_mem_kv[l], w_branch[l],
                              w_gate[l], b_gate[l], w_out[l], ln_mix_pre[l], ln_mix_post[l],
                              ln_ffn_pre[l], ln_ffn_post[l], w_gu[l], w_down[l])
        return x

    y_prompt = trunk(x_prompt, mem_prompt)
    y_sample = trunk(x_sample, mem_sample)
    return (y_prompt, y_sample)
```

```python
import numpy as np
import concourse.bass as bass
import concourse.mybir as mybir
from concourse.bass_utils import run_bass_kernel_spmd

F32 = mybir.dt.float32
BF16 = mybir.dt.bfloat16
AF = mybir.ActivationFunctionType
ALU = mybir.AluOpType

COMPUTE = ("tensor", "scalar", "vector", "gpsimd")
QUEUES = ("sync", "scalar", "gpsimd")
NDSEM = 10

D = 1024
DFF = 2816
NFF = 22
INW = 1696
H = 8
DK = 96
XH = 4
NMEM = 256
TT = 512
EPS = 1e-6
SBUF_BASE = 16640
SBUF_END = 229376


class Node:
    __slots__ = ("eng", "fn", "reads", "writes", "dma", "deps", "signal", "semval", "dsem", "idx", "fence")

    def __init__(self, eng, fn, reads, writes, dma):
        self.eng = eng
        self.fn = fn
        self.reads = reads
        self.writes = writes
        self.dma = dma
        self.deps = ()
        self.signal = False
        self.semval = 0
        self.dsem = None
        self.fence = False


class Sched:
    def __init__(self, nc):
        self.nc = nc
        self.nodes = []

    def op(self, eng, fn, reads=(), writes=()):
        n = Node(eng, fn, tuple(reads), tuple(writes), False)
        self.nodes.append(n)
        return n

    def dma(self, queue, fn, reads=(), writes=()):
        n = Node(queue, fn, tuple(reads), tuple(writes), True)
        self.nodes.append(n)
        return n

    def fence(self):
        n = Node(None, None, (), (), False)
        n.fence = True
        self.nodes.append(n)

    def emit(self):
        nc = self.nc
        last_w = {}
        readers = {}
        for i, n in enumerate(self.nodes):
            n.idx = i
            if n.fence:
                last_w.clear()
                readers.clear()
                continue
            deps = {}
            for k in n.reads:
                w = last_w.get(k)
                if w is not None:
                    deps[w.idx] = w
            for k in n.writes:
                w = last_w.get(k)
                if w is not None:
                    deps[w.idx] = w
                for r in readers.get(k, ()):
                    deps[r.idx] = r
            for k in n.reads:
                readers.setdefault(k, []).append(n)
            for k in n.writes:
                last_w[k] = n
                readers[k] = []
            deps.pop(i, None)
            dl = []
            for d in deps.values():
                if (not d.dma) and (not n.dma) and d.eng == n.eng and d.eng == "tensor":
                    continue
                dl.append(d)
            n.deps = dl
            for d in dl:
                d.signal = True
        ctx = []
        sems = {}
        for e in COMPUTE:
            g = nc.semaphore("s_" + e)
            sems[e] = g.__enter__()
            ctx.append(g)
        dsems = {}
        for q in QUEUES:
            for j in range(NDSEM):
                g = nc.semaphore("d_%s_%d" % (q, j))
                dsems[(q, j)] = g.__enter__()
                ctx.append(g)
        cnt = {e: 0 for e in COMPUTE}
        dcnt = {k: 0 for k in dsems}
        qrr = {q: 0 for q in QUEUES}
        last_node = {e: None for e in COMPUTE}
        for n in self.nodes:
            if n.fence:
                for e in COMPUTE:
                    if last_node[e] is not None:
                        last_node[e].signal = True
                continue
            if not n.dma:
                last_node[n.eng] = n
        for n in self.nodes:
            if n.fence:
                n.semval = (dict(cnt), dict(dcnt))
                continue
            if n.dma:
                j = qrr[n.eng]
                qrr[n.eng] = (j + 1) % NDSEM
                n.dsem = (n.eng, j)
                n.semval = (dcnt[n.dsem], dcnt[n.dsem] + 16)
                dcnt[n.dsem] += 16
            elif n.signal:
                cnt[n.eng] += 1
                n.semval = cnt[n.eng]
        final_dcnt = dict(dcnt)
        streams = {e: [] for e in ("sync", "scalar", "vector", "gpsimd", "tensor")}
        waited = {e: {} for e in streams}

        def add_wait(e, semkey, val):
            if val <= 0:
                return
            w = waited[e]
            if w.get(semkey, 0) >= val:
                return
            w[semkey] = val
            streams[e].append(("w", semkey, val))

        for n in self.nodes:
            if n.fence:
                c, d = n.semval
                for e in streams:
                    for ce in COMPUTE:
                        add_wait(e, ce, c[ce])
                    for dk, dv in d.items():
                        add_wait(e, dk, dv)
                continue
            e = n.eng
            for d in n.deps:
                if d.dma:
                    add_wait(e, d.dsem, d.semval[1])
                else:
                    add_wait(e, d.eng, d.semval)
            if n.dma:
                add_wait(e, n.dsem, n.semval[0])
            streams[e].append(("i", n))
        self.n_instr = {e: len(v) for e, v in streams.items()}
        nc_sems = dict(sems)
        nc_sems.update(dsems)

        def run(e, eng):
            for item in streams[e]:
                if item[0] == "w":
                    eng.wait_ge(nc_sems[item[1]], item[2])
                else:
                    n = item[1]
                    ins = n.fn(eng)
                    if n.dma:
                        ins.then_inc(dsems[n.dsem], 16)
                    elif n.signal:
                        ins.then_inc(sems[n.eng], 1)

        with nc.Block() as block:
            @block.sync
            def _(eng):
                run("sync", eng)
                for dk, dv in final_dcnt.items():
                    if dv > 0:
                        eng.wait_ge(dsems[dk], dv)

            @block.scalar
            def _(eng):
                run("scalar", eng)

            @block.vector
            def _(eng):
                run("vector", eng)

            @block.gpsimd
            def _(eng):
                run("gpsimd", eng)

            @block.tensor
            def _(eng):
                run("tensor", eng)
        for g in reversed(ctx):
            g.__exit__(None, None, None)


def build_nc(seq_lens, depth, debug=False):
    nc = bass.Bass("TRN2", target_bir_lowering=False)
    NSEQ = len(seq_lens)
    NTOK = sum(seq_lens)
    SMAX = max(seq_lens)
    seq_off = [sum(seq_lens[:i]) for i in range(NSEQ)]
    L = depth

    def din(name, shape, dt=F32):
        return nc.dram_tensor(name, list(shape), dt, kind="ExternalInput").ap()

    x_in = din("x", [NTOK, D])
    mem_in = din("mem", [NSEQ * NMEM, D])
    w_in = din("w_in", [L, D, INW])
    w_krp = din("w_krp", [L, D, 2 * DK])
    w_uq = din("w_uq", [L, 384, H * DK])
    w_uqB = din("w_uqB", [L, 384, H * DK])
    w_uk = din("w_uk", [L, 256, 512])
    w_uv = din("w_uv", [L, 256, 512])
    pool_mix = din("pool_mix", [L, 4, 128, 128])
    w_mem_kv = din("w_mem_kv", [L, D, 1024])
    w_branch = din("w_branch", [L, 3, 512, D])
    w_gate = din("w_gate", [L, D, 3 * D])
    w_out = din("w_out", [L, D, D])
    w_gu = din("w_gu", [L, D, 2 * DFF])
    w_down = din("w_down", [L, DFF, D])
    NV = 33
    vecs_in = din("vecs", [128, L * NV])
    bcv_in = din("bcv", [L * 5, D])
    cos_in = din("cos_t", [DK, SMAX])
    sin_in = din("sin_t", [DK, SMAX])
    ident_in = din("ident", [128, 128])
    pfix_in = din("pfix", [128, 2 * 4 * 8])

    y = nc.dram_tensor("y", [NTOK, D], F32, kind="ExternalOutput").ap()
    dbg_kind = "ExternalOutput" if debug else "Internal"

    def dscr(name, shape, dt):
        return nc.dram_tensor(name, list(shape), dt, kind=dbg_kind).ap()

    XM = dscr("XM", [NTOK, D], F32)
    HT = dscr("HT", [8, 128, NTOK], BF16)
    U = dscr("U", [4, 128, NTOK], F32)
    QT = dscr("QT", [H, DK, NTOK], BF16)
    KT = dscr("KT", [H, DK, NTOK], BF16)
    V = dscr("V", [NTOK, 512], BF16)
    MO = dscr("MO", [4, 128, NTOK], BF16)
    BO = dscr("BO", [4, 128, NTOK], BF16)
    WGUb = nc.dram_tensor("WGUb", [D, 2 * DFF], BF16).ap()
    WDNb = nc.dram_tensor("WDNb", [DFF, D], BF16).ap()

    S = Sched(nc)
    sb_cache = {}
    arena = [SBUF_BASE]

    def SB(name, shape, dt):
        nbytes = int(np.prod(shape[1:])) * (4 if dt == F32 else 2)
        nbytes = (nbytes + 31) // 32 * 32
        off = arena[0]
        arena[0] += nbytes
        assert arena[0] <= SBUF_END, ("SBUF overflow", name, arena[0])
        key = (name, off)
        if key not in sb_cache:
            sb_cache[key] = nc.alloc_sbuf_tensor_at("%s_%d" % (name, off), list(shape), dt, offset=off)
            assert tuple(sb_cache[key].shape) == tuple(shape)
        return sb_cache[key]

    pm = nc.alloc_psum_tensor("pm", [128, 8, 512], F32)
    pmb = pm[:, :, :].bitcast(BF16)
    bank_rr = [0]

    def bank():
        b = bank_rr[0]
        bank_rr[0] = (b + 1) % 8
        return b, "ps%d" % b

    def E(eng, meth, reads, writes, *a, **kw):
        return S.op(eng, lambda e: getattr(e, meth)(*a, **kw), reads, writes)

    def MM(out, lhsT, rhs, start, stop, reads, writes):
        return S.op("tensor", lambda e: e.matmul(out, lhsT=lhsT, rhs=rhs, start=start, stop=stop), reads, writes)

    def TR(out, in_, reads, writes):
        return S.op("tensor", lambda e: e.transpose(out=out, in_=in_, identity=ident_bf[:]), list(reads) + ["ident"], writes)

    def DMA(q, out, in_, reads, writes):
        return S.dma(q, lambda e: e.dma_start(out=out, in_=in_), reads, writes)

    def ACT(out, in_, func, reads, writes, **kw):
        return S.op("scalar", lambda e: e.activation(out=out, in_=in_, func=func, **kw), reads, writes)

    ident_f = SB("ident_f", [128, 128], F32)
    ident_bf = SB("ident_bf", [128, 128], BF16)
    ones_f = SB("ones_f", [128, 128], F32)
    ones_bf = SB("ones_bf", [128, 128], BF16)
    mhalf = SB("mhalf", [128, 512], F32)
    vecs = SB("vecs", [128, L * NV], F32)
    pfix = SB("pfix", [128, 2, 4, 8], F32)
    junk = SB("junk", [128, 1024], BF16)
    small = SB("small", [128, 64], F32)
    epsc = SB("epsc", [128, 8], F32)
    PERSIST_END = arena[0]

    DMA("sync", ident_f[:], ident_in[:, :], [], ["ident_f"])
    DMA("sync", vecs[:], vecs_in[:, :], [], ["vecs"])
    DMA("sync", pfix[:].rearrange("p a g t -> p (a g t)"), pfix_in[:, :], [], ["pfix"])
    E("vector", "tensor_copy", ["ident_f"], ["ident"], out=ident_bf[:], in_=ident_f[:])
    E("gpsimd", "memset", [], ["ones_f"], ones_f[:], 1.0)
    E("gpsimd", "memset", [], ["ones_bf"], ones_bf[:], 1.0)
    E("gpsimd", "memset", [], ["mhalf"], mhalf[:], -0.5)
    E("gpsimd", "memset", [], ["epsc"], epsc[:], EPS)
    S.fence()

    def vcol(l, j):
        return vecs[:, l * NV + j:l * NV + j + 1]

    def tiles_of():
        out = []
        for si, Sl in enumerate(seq_lens):
            for ti in range(Sl // TT):
                out.append((si, ti, seq_off[si] + ti * TT, ti * TT, Sl))
        return out

    ALLT = tiles_of()
    scale_mla = float(DK ** -0.5)
    scale_x = float(128 ** -0.5)

    def rstd_from_ms(ms_ap, n, key):
        E("vector", "tensor_scalar", [key], [key], out=ms_ap, in0=ms_ap, scalar1=EPS, scalar2=None, op0=ALU.add)
        E("gpsimd", "tensor_tensor", [key, "mhalf"], [key], out=ms_ap, in0=ms_ap, in1=mhalf[:, 0:n], op=ALU.pow)

    def tm_norm(xt, xkeys, gbc, gkey, hb, hbkey, st, stkey):
        for j in range(4):
            ACT(junk[:], xt[:, j, :], AF.Square, [xkeys[j]], [stkey, "junk"], scale=1.0 / 32.0, accum_out=st[:, j:j + 1])
        rstd_from_ms(st[:, 0:4], 4, stkey)
        for j in range(4):
            E("vector", "scalar_tensor_tensor", [xkeys[j], stkey, gkey], [hbkey + str(j)],
              out=hb[:, j, :], in0=xt[:, j, :], scalar=st[:, j:j + 1], in1=gbc[:], op0=ALU.mult, op1=ALU.mult)

    def transposes(hb, hbkey, hT, hTkey, nj=4):
        for c in range(8):
            b, bk = bank()
            for j in range(nj):
                TR(pmb[:, b, j * 128:(j + 1) * 128], hb[:, j, c * 128:(c + 1) * 128], [hbkey + str(j)], [bk])
            if c % 2 == 0:
                ACT(hT[:, c, :], pmb[:, b, 0:nj * 128], AF.Copy, [bk], [hTkey + str(c)])
            else:
                E("vector", "tensor_copy", [bk], [hTkey + str(c)], out=hT[:, c, :], in_=pmb[:, b, 0:nj * 128])

    def post_norm_res(banks, xap, xkey, gbc, gkey, st, stkey, tmp, tmpkeys, pq):
        c0 = 8 + 3 * pq
        sk = stkey + "q%d" % pq
        for n in range(2):
            b, bk = banks[n]
            ACT(junk[:, 0:512], pm[:, b, :], AF.Square, [bk], [sk + "p%d" % n, "junk"], scale=1.0 / 32.0,
                accum_out=st[:, c0 + n:c0 + n + 1])
        E("vector", "tensor_tensor", [sk + "p0", sk + "p1"], [sk + "r"], out=st[:, c0 + 2:c0 + 3], in0=st[:, c0:c0 + 1],
          in1=st[:, c0 + 1:c0 + 2], op=ALU.add)
        rstd_from_ms(st[:, c0 + 2:c0 + 3], 1, sk + "r")
        for n in range(2):
            b, bk = banks[n]
            E("vector", "scalar_tensor_tensor", [bk, sk + "r", gkey], [tmpkeys[n]],
              out=tmp[:, n * 512:(n + 1) * 512], in0=pm[:, b, :], scalar=st[:, c0 + 2:c0 + 3], in1=gbc[:, n * 512:(n + 1) * 512],
              op0=ALU.mult, op1=ALU.mult)
        E("gpsimd", "tensor_tensor", [xkey, tmpkeys[0], tmpkeys[1]], [xkey], out=xap, in0=xap,
          in1=tmp[:, 0:1024], op=ALU.add)

    for l in range(L):
        xsrc = x_in if l == 0 else y
        arena[0] = PERSIST_END
        win_sb = SB("A_win", [128, 8, INW], BF16)
        wkr_sb = SB("A_wkr", [128, 8, 2 * DK], BF16)
        wuqA = SB("A_wuqA", [128, 3, H * DK], BF16)
        wuqB = SB("A_wuqB", [128, 3, H * DK], BF16)
        wuk_sb = SB("A_wuk", [128, 2, 512], BF16)
        wuv_sb = SB("A_wuv", [128, 2, 512], BF16)
        gpre = SB("A_gpre", [128, D], F32)
        kmT = SB("A_kmT", [128, NSEQ, XH, NMEM], BF16)
        vm = SB("A_vm", [128, NSEQ, 2, 512], BF16)
        A_W_END = arena[0]
        wmkv = SB("A_wmkv", [128, 8, 1024], BF16)
        gmem = SB("A_gmem", [128, D], F32)
        memt = SB("A_memt", [128, 2, D], F32)
        memb = SB("A_memb", [128, 2, D], BF16)
        memT = SB("A_memT", [128, 8, NMEM], BF16)

        DMA("gpsimd", wmkv[:], w_mem_kv[l].rearrange("(k p) f -> p k f", p=128), [], ["wmkv"])
        for (c0, c1, nm) in ((512, 896, "cq"), (896, 1152, "ckv"), (0, 512, "u"), (1184, 1696, "qx")):
            DMA("gpsimd", win_sb[:, :, c0:c1], w_in[l, :, c0:c1].rearrange("(k p) f -> p k f", p=128), [], ["win_" + nm])
        DMA("gpsimd", wkr_sb[:], w_krp[l].rearrange("(k p) f -> p k f", p=128), [], ["wkr"])
        DMA("gpsimd", wuqA[:], w_uq[l].rearrange("(k p) f -> p k f", p=128), [], ["wuqA"])
        DMA("gpsimd", wuqB[:], w_uqB[l].rearrange("(k p) f -> p k f", p=128), [], ["wuqB"])
        DMA("gpsimd", wuk_sb[:], w_uk[l].rearrange("(k p) f -> p k f", p=128), [], ["wuk"])
        DMA("gpsimd", wuv_sb[:], w_uv[l].rearrange("(k p) f -> p k f", p=128), [], ["wuv"])
        DMA("sync", gpre[:], bcv_in[l * 5 + 0].partition_broadcast(128), [], ["gpre"])
        DMA("sync", gmem[:], bcv_in[l * 5 + 4].partition_broadcast(128), [], ["gmem"])
        for si in range(NSEQ):
            DMA("sync", memt[:], mem_in[si * NMEM:(si + 1) * NMEM, :].rearrange("(j p) d -> p j d", p=128), [], ["memt"])
            for j in range(2):
                ACT(junk[:], memt[:, j, :], AF.Square, ["memt"], ["mst", "junk"], scale=1.0 / 32.0, accum_out=small[:, j:j + 1])
            rstd_from_ms(small[:, 0:2], 2, "mst")
            for j in range(2):
                E("vector", "scalar_tensor_tensor", ["memt", "mst", "gmem"], ["memb%d" % j], out=memb[:, j, :],
                  in0=memt[:, j, :], scalar=small[:, j:j + 1], in1=gmem[:], op0=ALU.mult, op1=ALU.mult)
            transposes(memb, "memb", memT, "memT", nj=2)
            mk = ["memT%d" % c for c in range(8)]
            for h in range(XH):
                b, bk = bank()
                for c in range(8):
                    MM(pm[:, b, 0:NMEM], wmkv[:, c, h * 128:(h + 1) * 128], memT[:, c, :], c == 0, c == 7,
                       mk + ["wmkv"], [bk])
                ACT(kmT[:, si, h, :], pm[:, b, 0:NMEM], AF.Copy, [bk], ["kmT%d" % si])
            for j in range(2):
                b, bk = bank()
                for c in range(8):
                    MM(pm[:, b, :], memT[:, c, j * 128:(j + 1) * 128], wmkv[:, c, 512:1024], c == 0, c == 7,
                       mk + ["wmkv"], [bk])
                E("vector", "tensor_copy", [bk], ["vm%d" % si], out=vm[:, si, j, :], in_=pm[:, b, :])
        S.fence()
        arena[0] = A_W_END
        xtA = [SB("A_xt%d" % i, [128, 4, D], F32) for i in range(2)]
        hbA = [SB("A_hb%d" % i, [128, 4, D], BF16) for i in range(2)]
        hTA = [SB("A_hT%d" % i, [128, 8, TT], BF16) for i in range(2)]
        uTA = SB("A_uT", [128, 4, TT], F32)
        sqA = [SB("A_sq%d" % i, [128, TT], F32) for i in range(2)]
        rqA = SB("A_rq", [128, TT], F32)
        rkvA = SB("A_rkv", [128, TT], F32)
        cqn = SB("A_cqn", [128, 3, TT], BF16)
        ckvn = SB("A_ckvn", [128, 2, TT], BF16)
        qxT = SB("A_qxT", [128, 4, TT], BF16)
        cosA = [SB("A_cos%d" % i, [DK, TT], F32) for i in range(2)]
        sinA = [SB("A_sin%d" % i, [DK, TT], F32) for i in range(2)]
        t1A = [SB("A_t1%d" % i, [DK, TT], F32) for i in range(2)]
        t2A = [SB("A_t2%d" % i, [DK, TT], F32) for i in range(2)]
        QTt = SB("A_QTt", [DK, H, TT], BF16)
        KTt = SB("A_KTt", [DK, H, TT], BF16)
        Vt = SB("A_Vt", [128, 4, 512], BF16)
        PmT = [SB("A_PmT%d" % i, [128, 2, TT], BF16) for i in range(2)]
        recA = [SB("A_rec%d" % i, [128, TT], F32) for i in range(2)]
        moT = SB("A_moT", [128, 4, TT], BF16)
        stA = [SB("A_st%d" % i, [128, 16], F32) for i in range(2)]

        def A_loadx(i):
            si, ti, g0, p0, Sl = ALLT[i]
            par = i % 2
            for j in range(4):
                DMA("sync", xtA[par][:, j, :], xsrc[g0 + j * 128:g0 + (j + 1) * 128, :], [], ["Ax%d_%d" % (par, j)])

        def A_s1(i):
            si, ti, g0, p0, Sl = ALLT[i]
            par = i % 2
            xk = ["Ax%d_%d" % (par, j) for j in range(4)]
            DMA("sync", cosA[par][:], cos_in[:, p0:p0 + TT], [], ["Acos%d" % par])
            DMA("sync", sinA[par][:], sin_in[:, p0:p0 + TT], [], ["Asin%d" % par])
            tm_norm(xtA[par], xk, gpre, "gpre", hbA[par], "Ahb%d_" % par, stA[par], "Ast%d" % par)

        def A_s2(i):
            si, ti, g0, p0, Sl = ALLT[i]
            par = i % 2
            transposes(hbA[par], "Ahb%d_" % par, hTA[par], "AhT%d_" % par)
            DMA("sync", HT[:, :, g0:g0 + TT].rearrange("c p t -> p c t"), hTA[par][:],
                ["AhT%d_%d" % (par, c) for c in range(8)], [])

        def A_s3(i):
            si, ti, g0, p0, Sl = ALLT[i]
            par = i % 2
            hk = ["AhT%d_%d" % (par, c) for c in range(8)]
            hT = hTA[par]

            def zchunk(col0, m, wsb=None, wkeys=None, lhs=None):
                b, bk = bank()
                wk = wkeys or ["win_" + ("u" if col0 < 512 else "cq" if col0 < 896 else "ckv" if col0 < 1152 else "qx")]
                for c in range(8):
                    lt = lhs(c) if lhs is not None else win_sb[:, c, col0:col0 + m]
                    MM(pm[0:m, b, :], lt, hT[:, c, :], c == 0, c == 7, hk + wk, [bk])
                return b, bk

            cqb = []
            for c3 in range(3):
                b, bk = zchunk(512 + c3 * 128, 128)
                cqb.append((b, bk))
                ACT(sqA[c3 % 2][:], pm[:, b, :], AF.Square, [bk], ["Asq%d" % (c3 % 2)])
                if c3 == 0:
                    bo_, bok = bank()
                    onesb = (bo_, bok)
                MM(pm[:, onesb[0], :], ones_f[:], sqA[c3 % 2][:], c3 == 0, c3 == 2, ["ones_f", "Asq%d" % (c3 % 2)], [onesb[1]])
            ACT(rqA[:], pm[:, onesb[0], :], AF.Ln, [onesb[1], "epsc"], ["Arq"], scale=1.0 / 384.0, bias=epsc[:, 0:1])
            ACT(rqA[:], rqA[:], AF.Exp, ["Arq"], ["Arq"], scale=-0.5)
            for c3 in range(3):
                b, bk = cqb[c3]
                E("vector", "scalar_tensor_tensor", [bk, "Arq", "vecs"], ["Acqn"], out=cqn[:, c3, :], in0=pm[:, b, :],
                  scalar=vcol(l, c3), in1=rqA[:], op0=ALU.mult, op1=ALU.mult)
            ckb = []
            for c2 in range(2):
                b, bk = zchunk(896 + c2 * 128, 128)
                ckb.append((b, bk))
                ACT(sqA[c2 % 2][:], pm[:, b, :], AF.Square, [bk], ["Asq%d" % (c2 % 2)])
                if c2 == 0:
                    onesb = bank()
                MM(pm[:, onesb[0], :], ones_f[:], sqA[c2 % 2][:], c2 == 0, c2 == 1, ["ones_f", "Asq%d" % (c2 % 2)], [onesb[1]])
            ACT(rkvA[:], pm[:, onesb[0], :], AF.Ln, [onesb[1], "epsc"], ["Arkv"], scale=1.0 / 256.0, bias=epsc[:, 0:1])
            ACT(rkvA[:], rkvA[:], AF.Exp, ["Arkv"], ["Arkv"], scale=-0.5)
            for c2 in range(2):
                b, bk = ckb[c2]
                E("vector", "scalar_tensor_tensor", [bk, "Arkv", "vecs"], ["Ackvn"], out=ckvn[:, c2, :], in0=pm[:, b, :],
                  scalar=vcol(l, 3 + c2), in1=rkvA[:], op0=ALU.mult, op1=ALU.mult)
            bA, bAk = zchunk(0, DK, wkeys=["wkr"], lhs=lambda c: wkr_sb[:, c, 0:DK])
            bB, bBk = zchunk(0, DK, wkeys=["wkr"], lhs=lambda c: wkr_sb[:, c, DK:2 * DK])
            E("vector", "tensor_tensor", [bAk, "Acos%d" % par], ["At1_%d" % par], out=t1A[par][64:96, :], in0=pm[64:96, bA, :],
              in1=cosA[par][64:96, :], op=ALU.mult)
            E("vector", "tensor_tensor", [bBk, "Asin%d" % par], ["At2_%d" % par], out=t2A[par][64:96, :], in0=pm[64:96, bB, :],
              in1=sinA[par][64:96, :], op=ALU.mult)
            E("gpsimd", "tensor_tensor", ["At1_%d" % par, "At2_%d" % par], ["AKr0"], out=KTt[64:96, 0, :],
              in0=t1A[par][64:96, :], in1=t2A[par][64:96, :], op=ALU.add)
            for g in range(4):
                b, bk = zchunk(g * 128, 128)
                if g % 2 == 0:
                    ACT(uTA[:, g, :], pm[:, b, :], AF.Copy, [bk], ["AuT%d" % g])
                else:
                    E("vector", "tensor_copy", [bk], ["AuT%d" % g], out=uTA[:, g, :], in_=pm[:, b, :])
            DMA("sync", U[:, :, g0:g0 + TT].rearrange("g p t -> p g t"), uTA[:], ["AuT%d" % g for g in range(4)], [])
            for hx in range(4):
                b, bk = zchunk(1184 + hx * 128, 128)
                if hx % 2 == 0:
                    E("vector", "tensor_copy", [bk], ["Aqx%d" % hx], out=qxT[:, hx, :], in_=pm[:, b, :])
                else:
                    ACT(qxT[:, hx, :], pm[:, b, :], AF.Copy, [bk], ["Aqx%d" % hx])

        def A_s4(i):
            si, ti, g0, p0, Sl = ALLT[i]
            par = i % 2
            for h in range(H):
                bA, bAk = bank()
                for c in range(3):
                    MM(pm[0:DK, bA, :], wuqA[:, c, h * DK:(h + 1) * DK], cqn[:, c, :], c == 0, c == 2, ["wuqA", "Acqn"], [bAk])
                bB, bBk = bank()
                for c in range(3):
                    MM(pm[0:DK, bB, :], wuqB[:, c, h * DK:(h + 1) * DK], cqn[:, c, :], c == 0, c == 2, ["wuqB", "Acqn"], [bBk])
                tp = h % 2
                E("vector", "tensor_tensor", [bAk, "Acos%d" % par], ["At1_%d" % tp], out=t1A[tp][:, :], in0=pm[0:DK, bA, :],
                  in1=cosA[par][:, :], op=ALU.mult)
                E("vector", "tensor_tensor", [bBk, "Asin%d" % par], ["At2_%d" % tp], out=t2A[tp][:, :], in0=pm[0:DK, bB, :],
                  in1=sinA[par][:, :], op=ALU.mult)
                E("gpsimd", "tensor_tensor", ["At1_%d" % tp, "At2_%d" % tp], ["AQ%d" % h], out=QTt[:, h, :], in0=t1A[tp][:, :],
                  in1=t2A[tp][:, :], op=ALU.add)
            DMA("sync", QT[:, :, g0:g0 + TT].rearrange("h p t -> p h t"), QTt[:], ["AQ%d" % h for h in range(H)], [])
            for h in range(H):
                b, bk = bank()
                for c in range(2):
                    MM(pm[0:64, b, :], wuk_sb[:, c, h * 64:(h + 1) * 64], ckvn[:, c, :], c == 0, c == 1, ["wuk", "Ackvn"], [bk])
                ACT(KTt[0:64, h, :], pm[0:64, b, :], AF.Copy, [bk], ["AKn%d" % h])
            DMA("sync", KT[:, 0:64, g0:g0 + TT].rearrange("h p t -> p h t"), KTt[0:64, :, :],
                ["AKn%d" % h for h in range(H)], [])
            DMA("sync", KT[:, 64:96, g0:g0 + TT].rearrange("h p t -> p h t"),
                KTt[64:96, 0:1, :].to_broadcast([32, H, TT]), ["AKr0"], [])
            for j in range(4):
                b, bk = bank()
                for c in range(2):
                    MM(pm[:, b, :], ckvn[:, c, j * 128:(j + 1) * 128], wuv_sb[:, c, :], c == 0, c == 1, ["wuv", "Ackvn"], [bk])
                if j % 2 == 0:
                    E("vector", "tensor_copy", [bk], ["AV%d" % j], out=Vt[:, j, :], in_=pm[:, b, :])
                else:
                    ACT(Vt[:, j, :], pm[:, b, :], AF.Copy, [bk], ["AV%d" % j])
            DMA("sync", V[g0:g0 + TT, :].rearrange("(j p) f -> p j f", p=128), Vt[:], ["AV%d" % j for j in range(4)], [])
            def mQK(hx):
                pp = hx % 2
                for mt in range(2):
                    b, bk = bank()
                    MM(pm[:, b, :], kmT[:, si, hx, mt * 128:(mt + 1) * 128], qxT[:, hx, :], True, True,
                       ["kmT%d" % si, "Aqx%d" % hx], [bk])
                    ACT(PmT[pp][:, mt, :], pm[:, b, :], AF.Exp, [bk], ["APm%d_%d" % (pp, mt)], scale=scale_x)

            def mPV(hx):
                pp = hx % 2
                bo_, bok = bank()
                bd_, bdk = bank()
                for mt in range(2):
                    MM(pm[:, bo_, :], vm[:, si, mt, hx * 128:(hx + 1) * 128], PmT[pp][:, mt, :], mt == 0, mt == 1,
                       ["vm%d" % si, "APm%d_%d" % (pp, mt)], [bok])
                for mt in range(2):
                    MM(pm[:, bd_, :], ones_bf[:], PmT[pp][:, mt, :], mt == 0, mt == 1, ["ones_bf", "APm%d_%d" % (pp, mt)], [bdk])
                ACT(recA[pp][:], pm[:, bd_, :], AF.Ln, [bdk], ["Arec%d" % pp])
                ACT(recA[pp][:], recA[pp][:], AF.Exp, ["Arec%d" % pp], ["Arec%d" % pp], scale=-1.0)
                E("vector", "tensor_tensor", [bok, "Arec%d" % pp], ["Amo%d" % hx], out=moT[:, hx, :], in0=pm[:, bo_, :],
                  in1=recA[pp][:], op=ALU.mult)

            mQK(0)
            for hx in range(XH):
                if hx + 1 < XH:
                    mQK(hx + 1)
                mPV(hx)
            DMA("sync", MO[:, :, g0:g0 + TT].rearrange("g p t -> p g t"), moT[:], ["Amo%d" % hx for hx in range(XH)], [])

        NT = len(ALLT)
        A_loadx(0)
        if NT > 1:
            A_loadx(1)
        A_s1(0)
        A_s2(0)
        for i in range(NT):
            if i + 2 < NT:
                A_loadx(i + 2)
            if i + 1 < NT:
                A_s1(i + 1)
            A_s3(i)
            if i + 1 < NT:
                A_s2(i + 1)
            A_s4(i)
        S.fence()

        arena[0] = PERSIST_END
        wg_sb = SB("C_wg", [128, 8, 3 * D], BF16)
        wbr_sb = SB("C_wbr", [128, 12, D], BF16)
        wo_sb = SB("C_wo", [128, 8, D], BF16)
        mix_sb = SB("C_mix", [128, 4, 128], BF16)
        gpost = SB("C_gpost", [128, D], F32)
        for c in range(8):
            DMA("gpsimd", wg_sb[:, c, :], w_gate[l, c * 128:(c + 1) * 128, :], [], ["wg%d" % c])
        for n in range(3):
            DMA("gpsimd", wbr_sb[:, n * 4:(n + 1) * 4, :], w_branch[l, n].rearrange("(k p) f -> p k f", p=128), [], ["wbr%d" % n])
        DMA("gpsimd", wo_sb[:], w_out[l].rearrange("(k p) f -> p k f", p=128), [], ["wo"])
        DMA("gpsimd", mix_sb[:], pool_mix[l].rearrange("g c d -> c g d"), [], ["mix"])
        DMA("gpsimd", gpost[:], bcv_in[l * 5 + 1].partition_broadcast(128), [], ["gpost"])
        for c in range(8):
            DMA("gpsimd", WGUb[c * 128:(c + 1) * 128, :], w_gu[l, c * 128:(c + 1) * 128, :], [], [])
        for f0 in range(0, NFF, 2):
            DMA("gpsimd", WDNb[f0 * 128:(f0 + 2) * 128, :], w_down[l, f0 * 128:(f0 + 2) * 128, :], [], [])
        KTMAX = SMAX // 128
        KTs = [SB("B_KTs%d" % i, [DK, SMAX], BF16) for i in range(2)]
        QTs = [SB("B_QTs%d" % i, [DK, SMAX], BF16) for i in range(2)]
        Vsb = [SB("B_Vsb%d" % i, [128, KTMAX, 128], BF16) for i in range(2)]
        Pt = [SB("B_Pt%d" % i, [128, 2, TT], BF16) for i in range(3)]
        recB = [SB("B_rec%d" % i, [64, TT], F32) for i in range(2)]
        boB = [SB("B_bo%d" % i, [64, TT], BF16) for i in range(2)]
        for i in range(2):
            E("gpsimd", "memset", [], ["BVones%d" % i], Vsb[i][:, :, 64:128], 1.0)
        units = [(si, h) for si in range(NSEQ) for h in range(H)]

        def B_load(u):
            si, h = units[u]
            par = u % 2
            Sl = seq_lens[si]
            o = seq_off[si]
            DMA("sync", KTs[par][:, 0:Sl], KT[h, :, o:o + Sl], [], ["BK%d" % par])
            DMA("sync", QTs[par][:, 0:Sl], QT[h, :, o:o + Sl], [], ["BQ%d" % par])
            nkt = Sl // 128
            for k0 in range(0, nkt, 16):
                k1 = min(nkt, k0 + 16)
                DMA("sync", Vsb[par][:, k0:k1, 0:64],
                    V[o + k0 * 128:o + k1 * 128, h * 64:(h + 1) * 64].rearrange("(k p) f -> p k f", p=128),
                    [], ["BV%d_%d" % (par, k0)])

        sc_banks = [(0, 1), (2, 3), (4, 5)]
        flat = []
        qtc = 0
        for u, (si, h) in enumerate(units):
            Sl = seq_lens[si]
            nkt = Sl // 128
            ng = nkt // 2
            for qt in range(Sl // TT):
                for g in range(ng):
                    flat.append((u, qt, g, ng, nkt, qtc))
                qtc += 1
        NG = len(flat)

        def B_QK(k):
            u, qt, g, ng, nkt, qc = flat[k]
            par = u % 2
            s_ = k % 3
            q0 = qt * TT
            for t in range(2):
                b = sc_banks[s_][t]
                kt = 2 * g + t
                MM(pm[:, b, :], KTs[par][:, kt * 128:(kt + 1) * 128], QTs[par][:, q0:q0 + TT], True, True,
                   ["BK%d" % par, "BQ%d" % par], ["ps%d" % b])

        def B_EXP(k):
            s_ = k % 3
            b0 = sc_banks[s_][0]
            ACT(Pt[s_][:, :, :], pm[:, b0:b0 + 2, :], AF.Exp, ["ps%d" % b0, "ps%d" % (b0 + 1)], ["BPt%d" % s_],
                scale=scale_mla)

        def B_PV(k):
            u, qt, g, ng, nkt, qc = flat[k]
            si, h = units[u]
            par = u % 2
            s_ = k % 3
            ob = 6 + (qc % 2)
            obk = "ps%d" % ob
            kvkeys = ["BV%d_%d" % (par, k0) for k0 in range(0, nkt, 16)] + ["BVones%d" % par]
            for t in range(2):
                kt = 2 * g + t
                MM(pm[:, ob, :], Vsb[par][:, kt, :], Pt[s_][:, t, :], kt == 0, kt == nkt - 1, kvkeys + ["BPt%d" % s_], [obk])
            if g == ng - 1:
                op_ = qc % 2
                o = seq_off[si]
                q0 = qt * TT
                E("vector", "reciprocal", [obk], ["Brec%d" % op_], out=recB[op_][:], in_=pm[64:128, ob, :])
                E("vector", "tensor_tensor", [obk, "Brec%d" % op_], ["Bbo%d" % op_], out=boB[op_][:], in0=pm[0:64, ob, :],
                  in1=recB[op_][:], op=ALU.mult)
                c4 = h // 2
                r0 = (h % 2) * 64
                DMA("sync", BO[c4, r0:r0 + 64, o + q0:o + q0 + TT], boB[op_][:], ["Bbo%d" % op_], [])

        B_load(0)
        if len(units) > 1:
            B_load(1)
        for k in range(min(2, NG)):
            B_QK(k)
            B_EXP(k)
        for k in range(NG):
            if k + 2 < NG:
                B_QK(k + 2)
                B_EXP(k + 2)
            B_PV(k)
            if k + 1 < NG and flat[k + 1][0] != flat[k][0]:
                if flat[k][0] + 2 < len(units):
                    B_load(flat[k][0] + 2)
        S.fence()

        arena[0] = PERSIST_END
        wg_sb = SB("C_wg", [128, 8, 3 * D], BF16)
        wbr_sb = SB("C_wbr", [128, 12, D], BF16)
        wo_sb = SB("C_wo", [128, 8, D], BF16)
        mix_sb = SB("C_mix", [128, 4, 128], BF16)
        gpost = SB("C_gpost", [128, D], F32)
        hTC = [SB("C_hT%d" % i, [128, 8, TT], BF16) for i in range(2)]
        boC = [SB("C_bo%d" % i, [128, 4, TT], BF16) for i in range(2)]
        moC = [SB("C_mo%d" % i, [128, 4, TT], BF16) for i in range(2)]
        uh = [SB("C_uh%d" % i, [128, 4, TT + 16], F32) for i in range(2)]
        pa = SB("C_pa", [128, TT + 16], F32)
        pb = SB("C_pb", [128, TT + 16], F32)
        dT = SB("C_dT", [128, 4, TT], BF16)
        aT = SB("C_aT", [128, 4, TT], BF16)
        gs = [SB("C_gs%d" % i, [128, TT], F32) for i in range(3)]
        mt_ = [SB("C_m%d" % i, [128, TT], F32) for i in range(3)]
        mT = SB("C_mT", [128, 8, TT], BF16)
        xtC = SB("C_xt", [128, 4, D], F32)
        tmpC = [SB("C_tmp%d" % i, [128, D], F32) for i in range(2)]
        stC = SB("C_st", [128, 16], F32)
        fxC = SB("C_fx", [128, 16], F32)


        def C_load(i):
            si, ti, g0, p0, Sl = ALLT[i]
            par = i % 2
            DMA("sync", hTC[par][:], HT[:, :, g0:g0 + TT].rearrange("c p t -> p c t"), [], ["ChT%d" % par])
            DMA("sync", boC[par][:], BO[:, :, g0:g0 + TT].rearrange("c p t -> p c t"), [], ["Cbo%d" % par])
            DMA("sync", moC[par][:], MO[:, :, g0:g0 + TT].rearrange("c p t -> p c t"), [], ["Cmo%d" % par])
            lo = 8 if p0 == 0 else 0
            hi = TT + 8 if p0 + TT == Sl else TT + 16
            if lo > 0:
                E("gpsimd", "memset", [], ["Cuh%d" % par], uh[par][:, :, 0:8], 0.0)
            if hi < TT + 16:
                E("gpsimd", "memset", [], ["Cuh%d" % par], uh[par][:, :, TT + 8:TT + 16], 0.0)
            DMA("sync", uh[par][:, :, lo:hi], U[:, :, g0 - 8 + lo:g0 - 8 + hi].rearrange("g p t -> p g t"), [], ["Cuh%d" % par])

        def C_pool_g(i, g):
            si, ti, g0, p0, Sl = ALLT[i]
            par = i % 2
            uk = "Cuh%d" % par
            w = 2 << g
            lo2, hi2 = [(8, 520), (7, 521), (5, 523), (1, 527)][g]
            E("gpsimd", "tensor_tensor", [uk], ["Cpa"], out=pa[:, lo2:hi2], in0=uh[par][:, g, lo2 - 1:hi2 - 1],
              in1=uh[par][:, g, lo2:hi2], op=ALU.add)
            cur, curk, oth, othk = pa, "Cpa", pb, "Cpb"
            if g >= 1:
                lo4, hi4 = [(8, 520), (6, 522), (2, 526)][g - 1]
                E("gpsimd", "tensor_tensor", [curk], [othk], out=oth[:, lo4:hi4], in0=cur[:, lo4 - 1:hi4 - 1],
                  in1=cur[:, lo4 + 1:hi4 + 1], op=ALU.add)
                cur, curk, oth, othk = oth, othk, cur, curk
            if g >= 2:
                lo8, hi8 = [(8, 520), (4, 524)][g - 2]
                E("gpsimd", "tensor_tensor", [curk], [othk], out=oth[:, lo8:hi8], in0=cur[:, lo8 - 2:hi8 - 2],
                  in1=cur[:, lo8 + 2:hi8 + 2], op=ALU.add)
                cur, curk, oth, othk = oth, othk, cur, curk
            if g >= 3:
                E("gpsimd", "tensor_tensor", [curk], [othk], out=oth[:, 8:520], in0=cur[:, 4:516],
                  in1=cur[:, 12:524], op=ALU.add)
                cur, curk, oth, othk = oth, othk, cur, curk
            if p0 == 0:
                E("gpsimd", "tensor_tensor", [curk, "pfix"], ["Cfx0"], out=fxC[:, 0:8], in0=cur[:, 8:16],
                  in1=pfix[:, 0, g, :], op=ALU.mult)
            if p0 + TT == Sl:
                E("gpsimd", "tensor_tensor", [curk, "pfix"], ["Cfx1"], out=fxC[:, 8:16], in0=cur[:, 512:520],
                  in1=pfix[:, 1, g, :], op=ALU.mult)
            E("gpsimd", "tensor_tensor", [curk, "pfix"], [curk], out=cur[:, 8:520], in0=cur[:, 8:520],
              in1=pfix[:, 1, g, 0:1].to_broadcast([128, TT]), op=ALU.mult)
            if p0 == 0:
                E("gpsimd", "tensor_copy", ["Cfx0", curk], [curk], out=cur[:, 8:16], in_=fxC[:, 0:8])
            if p0 + TT == Sl:
                E("gpsimd", "tensor_copy", ["Cfx1", curk], [curk], out=cur[:, 512:520], in_=fxC[:, 8:16])
            E("gpsimd", "tensor_tensor", [curk, uk], ["CdT%d" % g], out=dT[:, g, :], in0=cur[:, 8:520],
              in1=uh[par][:, g, 8:520], op=ALU.subtract)

        def C_poolmix(i):
            for g in range(4):
                b, bk = bank()
                MM(pm[:, b, :], mix_sb[:, g, :], dT[:, g, :], True, True, ["mix", "CdT%d" % g], [bk])
                ACT(aT[:, g, :], pm[:, b, :], AF.Copy, [bk, "vecs"], ["CaT%d" % g], scale=vcol(l, 5 + g))

        def C_main(i):
            si, ti, g0, p0, Sl = ALLT[i]
            par = i % 2
            for j in range(4):
                DMA("sync", xtC[:, j, :], xsrc[g0 + j * 128:g0 + (j + 1) * 128, :], [], ["Cx%d" % j])
            srcs = [(aT, ["CaT%d" % g for g in range(4)]), (boC[par], ["Cbo%d" % par]), (moC[par], ["Cmo%d" % par])]
            for jc in range(8):
                gb = []
                for n in range(3):
                    b, bk = bank()
                    for c in range(8):
                        MM(pm[:, b, :], wg_sb[:, c, n * D + jc * 128:n * D + (jc + 1) * 128], hTC[par][:, c, :], c == 0, c == 7,
                           ["wg%d" % c, "ChT%d" % par], [bk])
                    ACT(gs[n][:], pm[:, b, :], AF.Sigmoid, [bk, "vecs"], ["Cgs%d" % n], bias=vcol(l, 9 + n * 8 + jc))
                for n in range(3):
                    b, bk = bank()
                    src, sk = srcs[n]
                    for c in range(4):
                        MM(pm[:, b, :], wbr_sb[:, n * 4 + c, jc * 128:(jc + 1) * 128], src[:, c, :], c == 0, c == 3,
                           ["wbr%d" % n] + sk, [bk])
                    E("vector", "tensor_tensor", [bk, "Cgs%d" % n], ["Cm%d" % n], out=mt_[n][:], in0=pm[:, b, :], in1=gs[n][:],
                      op=ALU.mult)
                E("vector", "tensor_tensor", ["Cm0", "Cm1"], ["Cm0"], out=mt_[0][:], in0=mt_[0][:], in1=mt_[1][:], op=ALU.add)
                E("vector", "tensor_tensor", ["Cm0", "Cm2"], ["CmT%d" % jc], out=mT[:, jc, :], in0=mt_[0][:], in1=mt_[2][:],
                  op=ALU.add)
                if jc < 4 and i + 1 < NT:
                    C_pool_g(i + 1, 3 - jc)
            if i + 1 < NT:
                C_poolmix(i + 1)
            mk = ["CmT%d" % jc for jc in range(8)]
            for j in range(4):
                banks = []
                for n in range(2):
                    b, bk = bank()
                    banks.append((b, bk))
                    for c in range(8):
                        MM(pm[:, b, :], mT[:, c, j * 128:(j + 1) * 128], wo_sb[:, c, n * 512:(n + 1) * 512], c == 0, c == 7,
                           mk + ["wo"], [bk])
                post_norm_res(banks, xtC[:, j, :], "Cx%d" % j, gpost, "gpost", stC, "Cst", tmpC[j % 2], ["Ctmp%d_0" % (j % 2), "Ctmp%d_1" % (j % 2)], j % 2)
                DMA("sync", XM[g0 + j * 128:g0 + (j + 1) * 128, :], xtC[:, j, :], ["Cx%d" % j], [])

        C_load(0)
        for g in range(4):
            C_pool_g(0, g)
        C_poolmix(0)
        for i in range(NT):
            if i + 1 < NT:
                C_load(i + 1)
            C_main(i)
        S.fence()

        arena[0] = PERSIST_END
        wgu_sb = SB("D_wgu", [128, 8, 2 * DFF], BF16)
        wdn_sb = SB("D_wdn", [128, NFF, D], BF16)
        gfpre = SB("D_gfpre", [128, D], F32)
        gfpost = SB("D_gfpost", [128, D], F32)
        xinD = [SB("D_xin%d" % i, [128, D], F32) for i in range(2)]
        hbD = [SB("D_hb%d" % i, [128, D], BF16) for i in range(2)]
        hTD = SB("D_hT", [128, 8, TT], BF16)
        actT = SB("D_actT", [128, NFF, TT], BF16)
        sgD2 = SB("D_sg2", [128, 2, TT], F32)
        sgD = [sgD2[:, 0, :], sgD2[:, 1, :]]
        tmpD = SB("D_tmp", [128, D], F32)
        xrD = [SB("D_xr%d" % i, [128, D], F32) for i in range(2)]
        stD = SB("D_st", [128, 16], F32)
        DMA("sync", gfpre[:], bcv_in[l * 5 + 2].partition_broadcast(128), [], ["gfpre"])
        DMA("sync", gfpost[:], bcv_in[l * 5 + 3].partition_broadcast(128), [], ["gfpost"])

        def D_weights():
            for fb in range(NFF // 2):
                for half, nm in ((0, "g"), (1, "u")):
                    c0 = half * DFF + fb * 256
                    DMA("sync", wgu_sb[:, :, c0:c0 + 256], WGUb[:, c0:c0 + 256].rearrange("(k p) f -> p k f", p=128), [],
                        ["wgu_%s%d" % (nm, fb)])
            for f0 in range(0, NFF, 2):
                DMA("sync", wdn_sb[:, f0:f0 + 2, :], WDNb[f0 * 128:(f0 + 2) * 128, :].rearrange("(k p) f -> p k f", p=128),
                    [], ["wdn%d" % f0, "wdn%d" % (f0 + 1)])

        def D_norm(i, j):
            g0 = ALLT[i][2]
            q = j % 2
            DMA("sync", xinD[q][:], XM[g0 + j * 128:g0 + (j + 1) * 128, :], [], ["Dxin%d" % q])
            ACT(junk[:], xinD[q][:], AF.Square, ["Dxin%d" % q], ["Dst%d" % j, "junk"], scale=1.0 / 32.0, accum_out=stD[:, j:j + 1])
            rstd_from_ms(stD[:, j:j + 1], 1, "Dst%d" % j)
            E("vector", "scalar_tensor_tensor", ["Dxin%d" % q, "Dst%d" % j, "gfpre"], ["Dhb%d" % q], out=hbD[q][:],
              in0=xinD[q][:], scalar=stD[:, j:j + 1], in1=gfpre[:], op0=ALU.mult, op1=ALU.mult)

        def D_tr(i, j):
            q = j % 2
            b, bk = bank()
            for c in range(8):
                TR(pmb[:, b, c * 128:(c + 1) * 128], hbD[q][:, c * 128:(c + 1) * 128], ["Dhb%d" % q], [bk])
            src = pmb[:, b, 0:1024].rearrange("p (c t) -> p c t", c=8)
            if j % 2 == 0:
                ACT(hTD[:, :, j * 128:(j + 1) * 128], src, AF.Copy, [bk], ["DhT_%d" % j])
            else:
                E("vector", "tensor_copy", [bk], ["DhT_%d" % j], out=hTD[:, :, j * 128:(j + 1) * 128], in_=src)

        def D_s3(i):
            hk = ["DhT_%d" % j for j in range(4)]
            for f in range(NFF):
                bg, bgk = bank()
                for c in range(8):
                    MM(pm[:, bg, :], wgu_sb[:, c, f * 128:(f + 1) * 128], hTD[:, c, :], c == 0, c == 7, hk + ["wgu_g%d" % (f // 2)], [bgk])
                bu, buk = bank()
                for c in range(8):
                    MM(pm[:, bu, :], wgu_sb[:, c, DFF + f * 128:DFF + (f + 1) * 128], hTD[:, c, :], c == 0, c == 7,
                       hk + ["wgu_u%d" % (f // 2)], [buk])
                ACT(sgD[f % 2], pm[:, bg, :], AF.Silu, [bgk], ["Dsg%d" % (f % 2)])
                E("vector", "tensor_tensor", [buk, "Dsg%d" % (f % 2)], ["Dact%d" % f], out=actT[:, f, :], in0=pm[:, bu, :],
                  in1=sgD[f % 2], op=ALU.mult)

        def D_down(i, j):
            g0 = ALLT[i][2]
            q = j % 2
            ak = ["Dact%d" % f for f in range(NFF)]
            DMA("sync", xrD[q][:], XM[g0 + j * 128:g0 + (j + 1) * 128, :], [], ["Dxr%d" % q])
            banks = []
            for n in range(2):
                b, bk = bank()
                banks.append((b, bk))
                for f in range(NFF):
                    MM(pm[:, b, :], actT[:, f, j * 128:(j + 1) * 128], wdn_sb[:, f, n * 512:(n + 1) * 512], f == 0,
                       f == NFF - 1, ak + ["wdn%d" % f], [bk])
            if q == 0:
                post_norm_res(banks, xrD[q][:], "Dxr%d" % q, gfpost, "gfpost", stD, "Dst2", tmpD, ["Dtmp0", "Dtmp1"], 0)
            else:
                post_norm_res(banks, xrD[q][:], "Dxr%d" % q, gfpost, "gfpost", stD, "Dst2", sgD2[:].rearrange("p a t -> p (a t)"),
                              ["Dsg0", "Dsg1"], 1)
            DMA("sync", y[g0 + j * 128:g0 + (j + 1) * 128, :], xrD[q][:], ["Dxr%d" % q], [])

        for j in range(4):
            D_norm(0, j)
            D_tr(0, j)
        D_weights()
        for i in range(NT):
            D_s3(i)
            nx = i + 1 < NT
            if nx:
                D_norm(i + 1, 0)
                D_norm(i + 1, 1)
            D_down(i, 0)
            D_down(i, 1)
            if nx:
                D_tr(i + 1, 0)
                D_tr(i + 1, 1)
                D_norm(i + 1, 2)
                D_norm(i + 1, 3)
            D_down(i, 2)
            if nx:
                D_tr(i + 1, 2)
                D_tr(i + 1, 3)
            D_down(i, 3)
        S.fence()

    S.emit()
    return nc, S


def host_consts(smax):
    inv = 1.0 / (10000.0 ** (np.arange(0, 32, 2, dtype=np.float32) / 32))
    ang = np.arange(smax, dtype=np.float32)[:, None] * inv[None, :]
    cos = np.cos(ang).astype(np.float32).T
    sin = np.sin(ang).astype(np.float32).T
    cos_t = np.ones((DK, smax), np.float32)
    sin_t = np.zeros((DK, smax), np.float32)
    cos_t[64:80] = cos
    cos_t[80:96] = cos
    sin_t[64:80] = -sin
    sin_t[80:96] = sin
    pf = np.zeros((2, 4, 8), np.float32)
    for g in range(4):
        w = 2 << g
        half = w // 2
        for t in range(8):
            pf[0, g, t] = 1.0 / (t + half) if t < half else 1.0 / w
            fe = 8 - t
            pf[1, g, t] = 1.0 / (fe + half) if fe < half else 1.0 / w
    pfix = np.broadcast_to(pf.reshape(1, -1), (128, 64)).copy()
    return cos_t, sin_t, np.eye(128, dtype=np.float32), pfix


def host_weights(w, L):
    f = lambda a: np.ascontiguousarray(np.asarray(a, dtype=np.float32))
    w_in = f(w["w_in"])
    w_krp = np.zeros((L, D, 2, DK), np.float32)
    w_krp[:, :, 0, 64:96] = w_in[:, :, 1152:1184]
    w_krp[:, :, 1, 64:80] = w_in[:, :, 1168:1184]
    w_krp[:, :, 1, 80:96] = w_in[:, :, 1152:1168]
    w_uq = f(w["w_uq"])
    wq4 = w_uq.reshape(L, 384, H, DK)
    w_uqB = np.zeros((L, 384, H, DK), np.float32)
    w_uqB[:, :, :, 64:80] = wq4[:, :, :, 80:96]
    w_uqB[:, :, :, 80:96] = wq4[:, :, :, 64:80]
    NV = 33
    vecs = np.zeros((128, L, NV), np.float32)
    vecs[:, :, 0:3] = f(w["q_norm"]).reshape(L, 3, 128).transpose(2, 0, 1)
    vecs[:, :, 3:5] = f(w["kv_norm"]).reshape(L, 2, 128).transpose(2, 0, 1)
    vecs[:, :, 5:9] = f(w["pool_scale"]).reshape(L, 4, 128).transpose(2, 0, 1)
    vecs[:, :, 9:33] = f(w["b_gate"]).reshape(L, 24, 128).transpose(2, 0, 1)
    bcv = np.stack([f(w["ln_mix_pre"]), f(w["ln_mix_post"]), f(w["ln_ffn_pre"]), f(w["ln_ffn_post"]), f(w["mem_norm"])],
                   axis=1).reshape(L * 5, D)
    return {
        "w_in": w_in, "w_krp": w_krp.reshape(L, D, 2 * DK), "w_uq": w_uq, "w_uqB": w_uqB.reshape(L, 384, H * DK),
        "w_uk": f(w["w_uk"]), "w_uv": f(w["w_uv"]), "pool_mix": f(w["pool_mix"]), "w_mem_kv": f(w["w_mem_kv"]),
        "w_branch": f(w["w_branch"]), "w_gate": f(w["w_gate"]), "w_out": f(w["w_out"]), "w_gu": f(w["w_gu"]),
        "w_down": f(w["w_down"]), "vecs": np.ascontiguousarray(vecs.reshape(128, L * NV)), "bcv": np.ascontiguousarray(bcv),
    }


_NC_CACHE = {}


def run_cores(xs, mems, weights, seq_lens, depth, debug=False):
    key = (tuple(seq_lens), depth, debug)
    if key not in _NC_CACHE:
        _NC_CACHE[key] = build_nc(seq_lens, depth, debug)
    nc, _ = _NC_CACHE[key]
    hw = host_weights(weights, depth)
    cos_t, sin_t, ident, pfix = host_consts(max(seq_lens))
    base = dict(hw)
    base.update({"cos_t": cos_t, "sin_t": sin_t, "ident": ident, "pfix": pfix})
    in_maps = []
    for xc, mc in zip(xs, mems):
        m = dict(base)
        m["x"] = np.ascontiguousarray(xc, dtype=np.float32)
        m["mem"] = np.ascontiguousarray(mc, dtype=np.float32)
        in_maps.append(m)
    res = run_bass_kernel_spmd(nc, in_maps, core_ids=list(range(len(xs))))
    return res.results


def kernel(x_prompt, x_sample, mem_prompt, mem_sample, **weights):
    x_prompt = np.asarray(x_prompt)
    x_sample = np.asarray(x_sample)
    mem_prompt = np.asarray(mem_prompt)
    mem_sample = np.asarray(mem_sample)
    B, SP, _ = x_prompt.shape
    B2, SS, _ = x_sample.shape
    n = 8
    seq_lens = (SP, SS, SS)
    depth = int(np.asarray(weights["w_in"]).shape[0])
    xs, mems = [], []
    for c in range(n):
        xs.append(np.concatenate([x_prompt[c], x_sample[2 * c], x_sample[2 * c + 1]], axis=0))
        mems.append(np.concatenate([mem_prompt[c], mem_sample[2 * c], mem_sample[2 * c + 1]], axis=0))
    results = run_cores(xs, mems, weights, seq_lens, depth)
    y_prompt = np.empty((B, SP, D), np.float32)
    y_sample = np.empty((B2, SS, D), np.float32)
    for c in range(n):
        yc = results[c]["y"]
        y_prompt[c] = yc[0:SP]
        y_sample[2 * c] = yc[SP:SP + SS]
        y_sample[2 * c + 1] = yc[SP + SS:SP + 2 * SS]
    return (y_prompt, y_sample)
```

```python
import numpy as np
import concourse.bass as bass
import concourse.mybir as mybir
from concourse.bass_utils import run_bass_kernel_spmd

F32 = mybir.dt.float32
BF16 = mybir.dt.bfloat16
AF = mybir.ActivationFunctionType
ALU = mybir.AluOpType

COMPUTE = ("tensor", "scalar", "vector", "gpsimd")
QUEUES = ("sync", "scalar", "gpsimd")
NDSEM = 10

D = 1024
DFF = 2816
NFF = 22
INW = 1696
H = 8
DK = 96
XH = 4
NMEM = 256
TT = 512
EPS = 1e-6
SBUF_BASE = 16640
SBUF_END = 229376


class Node:
    __slots__ = ("eng", "fn", "reads", "writes", "dma", "deps", "signal", "semval", "dsem", "idx", "fence")

    def __init__(self, eng, fn, reads, writes, dma):
        self.eng = eng
        self.fn = fn
        self.reads = reads
        self.writes = writes
        self.dma = dma
        self.deps = ()
        self.signal = False
        self.semval = 0
        self.dsem = None
        self.fence = False


class Sched:
    def __init__(self, nc):
        self.nc = nc
        self.nodes = []

    def op(self, eng, fn, reads=(), writes=()):
        n = Node(eng, fn, tuple(reads), tuple(writes), False)
        self.nodes.append(n)
        return n

    def dma(self, queue, fn, reads=(), writes=()):
        n = Node(queue, fn, tuple(reads), tuple(writes), True)
        self.nodes.append(n)
        return n

    def fence(self):
        n = Node(None, None, (), (), False)
        n.fence = True
        self.nodes.append(n)

    def emit(self):
        nc = self.nc
        last_w = {}
        readers = {}
        for i, n in enumerate(self.nodes):
            n.idx = i
            if n.fence:
                last_w.clear()
                readers.clear()
                continue
            deps = {}
            for k in n.reads:
                w = last_w.get(k)
                if w is not None:
                    deps[w.idx] = w
            for k in n.writes:
                w = last_w.get(k)
                if w is not None:
                    deps[w.idx] = w
                for r in readers.get(k, ()):
                    deps[r.idx] = r
            for k in n.reads:
                readers.setdefault(k, []).append(n)
            for k in n.writes:
                last_w[k] = n
                readers[k] = []
            deps.pop(i, None)
            dl = []
            for d in deps.values():
                if (not d.dma) and (not n.dma) and d.eng == n.eng and d.eng == "tensor":
                    continue
                dl.append(d)
            n.deps = dl
            for d in dl:
                d.signal = True
        ctx = []
        sems = {}
        for e in COMPUTE:
            g = nc.semaphore("s_" + e)
            sems[e] = g.__enter__()
            ctx.append(g)
        dsems = {}
        for q in QUEUES:
            for j in range(NDSEM):
                g = nc.semaphore("d_%s_%d" % (q, j))
                dsems[(q, j)] = g.__enter__()
                ctx.append(g)
        cnt = {e: 0 for e in COMPUTE}
        dcnt = {k: 0 for k in dsems}
        qrr = {q: 0 for q in QUEUES}
        last_node = {e: None for e in COMPUTE}
        for n in self.nodes:
            if n.fence:
                for e in COMPUTE:
                    if last_node[e] is not None:
                        last_node[e].signal = True
                continue
            if not n.dma:
                last_node[n.eng] = n
        for n in self.nodes:
            if n.fence:
                n.semval = (dict(cnt), dict(dcnt))
                continue
            if n.dma:
                j = qrr[n.eng]
                qrr[n.eng] = (j + 1) % NDSEM
                n.dsem = (n.eng, j)
                n.semval = (dcnt[n.dsem], dcnt[n.dsem] + 16)
                dcnt[n.dsem] += 16
            elif n.signal:
                cnt[n.eng] += 1
                n.semval = cnt[n.eng]
        final_dcnt = dict(dcnt)
        streams = {e: [] for e in ("sync", "scalar", "vector", "gpsimd", "tensor")}
        waited = {e: {} for e in streams}

        def add_wait(e, semkey, val):
            if val <= 0:
                return
            w = waited[e]
            if w.get(semkey, 0) >= val:
                return
            w[semkey] = val
            streams[e].append(("w", semkey, val))

        for n in self.nodes:
            if n.fence:
                c, d = n.semval
                for e in streams:
                    for ce in COMPUTE:
                        add_wait(e, ce, c[ce])
                    for dk, dv in d.items():
                        add_wait(e, dk, dv)
                continue
            e = n.eng
            for d in n.deps:
                if d.dma:
                    add_wait(e, d.dsem, d.semval[1])
                else:
                    add_wait(e, d.eng, d.semval)
            if n.dma:
                add_wait(e, n.dsem, n.semval[0])
            streams[e].append(("i", n))
        self.n_instr = {e: len(v) for e, v in streams.items()}
        nc_sems = dict(sems)
        nc_sems.update(dsems)

        def run(e, eng):
            for item in streams[e]:
                if item[0] == "w":
                    eng.wait_ge(nc_sems[item[1]], item[2])
                else:
                    n = item[1]
                    ins = n.fn(eng)
                    if n.dma:
                        ins.then_inc(dsems[n.dsem], 16)
                    elif n.signal:
                        ins.then_inc(sems[n.eng], 1)

        with nc.Block() as block:
            @block.sync
            def _(eng):
                run("sync", eng)
                for dk, dv in final_dcnt.items():
                    if dv > 0:
                        eng.wait_ge(dsems[dk], dv)

            @block.scalar
            def _(eng):
                run("scalar", eng)

            @block.vector
            def _(eng):
                run("vector", eng)

            @block.gpsimd
            def _(eng):
                run("gpsimd", eng)

            @block.tensor
            def _(eng):
                run("tensor", eng)
        for g in reversed(ctx):
            g.__exit__(None, None, None)


def build_nc(seq_lens, depth, debug=False):
    nc = bass.Bass("TRN2", target_bir_lowering=False)
    NSEQ = len(seq_lens)
    NTOK = sum(seq_lens)
    SMAX = max(seq_lens)
    seq_off = [sum(seq_lens[:i]) for i in range(NSEQ)]
    L = depth

    def din(name, shape, dt=F32):
        return nc.dram_tensor(name, list(shape), dt, kind="ExternalInput").ap()

    x_in = din("x", [NTOK, D])
    mem_in = din("mem", [NSEQ * NMEM, D])
    w_in = din("w_in", [L, D, INW])
    w_krp = din("w_krp", [L, D, 2 * DK])
    w_uq = din("w_uq", [L, 384, H * DK])
    w_uqB = din("w_uqB", [L, 384, H * DK])
    w_uk = din("w_uk", [L, 256, 512])
    w_uv = din("w_uv", [L, 256, 512])
    pool_mix = din("pool_mix", [L, 4, 128, 128])
    w_mem_kv = din("w_mem_kv", [L, D, 1024])
    w_branch = din("w_branch", [L, 3, 512, D])
    w_gate = din("w_gate", [L, D, 3 * D])
    w_out = din("w_out", [L, D, D])
    w_gu = din("w_gu", [L, D, 2 * DFF])
    w_down = din("w_down", [L, DFF, D])
    NV = 33
    vecs_in = din("vecs", [128, L * NV])
    bcv_in = din("bcv", [L * 5, D])
    cos_in = din("cos_t", [DK, SMAX])
    sin_in = din("sin_t", [DK, SMAX])
    ident_in = din("ident", [128, 128])
    pfix_in = din("pfix", [128, 2 * 4 * 8])

    y = nc.dram_tensor("y", [NTOK, D], F32, kind="ExternalOutput").ap()
    dbg_kind = "ExternalOutput" if debug else "Internal"

    def dscr(name, shape, dt):
        return nc.dram_tensor(name, list(shape), dt, kind=dbg_kind).ap()

    XM = dscr("XM", [NTOK, D], F32)
    HT = dscr("HT", [8, 128, NTOK], BF16)
    U = dscr("U", [4, 128, NTOK], F32)
    QT = dscr("QT", [H, DK, NTOK], BF16)
    KT = dscr("KT", [H, DK, NTOK], BF16)
    V = dscr("V", [NTOK, 512], BF16)
    MO = dscr("MO", [4, 128, NTOK], BF16)
    BO = dscr("BO", [4, 128, NTOK], BF16)

    S = Sched(nc)
    sb_cache = {}
    arena = [SBUF_BASE]

    def SB(name, shape, dt):
        nbytes = int(np.prod(shape[1:])) * (4 if dt == F32 else 2)
        nbytes = (nbytes + 31) // 32 * 32
        off = arena[0]
        arena[0] += nbytes
        assert arena[0] <= SBUF_END, ("SBUF overflow", name, arena[0])
        key = (name, off)
        if key not in sb_cache:
            sb_cache[key] = nc.alloc_sbuf_tensor_at("%s_%d" % (name, off), list(shape), dt, offset=off)
            assert tuple(sb_cache[key].shape) == tuple(shape)
        return sb_cache[key]

    pm = nc.alloc_psum_tensor("pm", [128, 8, 512], F32)
    pmb = pm[:, :, :].bitcast(BF16)
    bank_rr = [0]

    def bank():
        b = bank_rr[0]
        bank_rr[0] = (b + 1) % 8
        return b, "ps%d" % b

    def E(eng, meth, reads, writes, *a, **kw):
        return S.op(eng, lambda e: getattr(e, meth)(*a, **kw), reads, writes)

    def MM(out, lhsT, rhs, start, stop, reads, writes):
        return S.op("tensor", lambda e: e.matmul(out, lhsT=lhsT, rhs=rhs, start=start, stop=stop), reads, writes)

    def TR(out, in_, reads, writes):
        return S.op("tensor", lambda e: e.transpose(out=out, in_=in_, identity=ident_bf[:]), list(reads) + ["ident"], writes)

    def DMA(q, out, in_, reads, writes):
        return S.dma(q, lambda e: e.dma_start(out=out, in_=in_), reads, writes)

    def ACT(out, in_, func, reads, writes, **kw):
        return S.op("scalar", lambda e: e.activation(out=out, in_=in_, func=func, **kw), reads, writes)

    ident_f = SB("ident_f", [128, 128], F32)
    ident_bf = SB("ident_bf", [128, 128], BF16)
    ones_f = SB("ones_f", [128, 128], F32)
    ones_bf = SB("ones_bf", [128, 128], BF16)
    mhalf = SB("mhalf", [128, 512], F32)
    vecs = SB("vecs", [128, L * NV], F32)
    pfix = SB("pfix", [128, 2, 4, 8], F32)
    junk = SB("junk", [128, 1024], BF16)
    small = SB("small", [128, 64], F32)
    epsc = SB("epsc", [128, 8], F32)
    PERSIST_END = arena[0]

    DMA("sync", ident_f[:], ident_in[:, :], [], ["ident_f"])
    DMA("sync", vecs[:], vecs_in[:, :], [], ["vecs"])
    DMA("sync", pfix[:].rearrange("p a g t -> p (a g t)"), pfix_in[:, :], [], ["pfix"])
    E("vector", "tensor_copy", ["ident_f"], ["ident"], out=ident_bf[:], in_=ident_f[:])
    E("gpsimd", "memset", [], ["ones_f"], ones_f[:], 1.0)
    E("gpsimd", "memset", [], ["ones_bf"], ones_bf[:], 1.0)
    E("gpsimd", "memset", [], ["mhalf"], mhalf[:], -0.5)
    E("gpsimd", "memset", [], ["epsc"], epsc[:], EPS)
    S.fence()

    def vcol(l, j):
        return vecs[:, l * NV + j:l * NV + j + 1]

    def tiles_of():
        out = []
        for si, Sl in enumerate(seq_lens):
            for ti in range(Sl // TT):
                out.append((si, ti, seq_off[si] + ti * TT, ti * TT, Sl))
        return out

    ALLT = tiles_of()
    scale_mla = float(DK ** -0.5)
    scale_x = float(128 ** -0.5)

    def rstd_from_ms(ms_ap, n, key):
        E("vector", "tensor_scalar", [key], [key], out=ms_ap, in0=ms_ap, scalar1=EPS, scalar2=None, op0=ALU.add)
        E("gpsimd", "tensor_tensor", [key, "mhalf"], [key], out=ms_ap, in0=ms_ap, in1=mhalf[:, 0:n], op=ALU.pow)

    def tm_norm(xt, xkeys, gbc, gkey, hb, hbkey, st, stkey):
        for j in range(4):
            ACT(junk[:], xt[:, j, :], AF.Square, [xkeys[j]], [stkey, "junk"], scale=1.0 / 32.0, accum_out=st[:, j:j + 1])
        rstd_from_ms(st[:, 0:4], 4, stkey)
        for j in range(4):
            E("vector", "scalar_tensor_tensor", [xkeys[j], stkey, gkey], [hbkey + str(j)],
              out=hb[:, j, :], in0=xt[:, j, :], scalar=st[:, j:j + 1], in1=gbc[:], op0=ALU.mult, op1=ALU.mult)

    def transposes(hb, hbkey, hT, hTkey, nj=4):
        for c in range(8):
            b, bk = bank()
            for j in range(nj):
                TR(pmb[:, b, j * 128:(j + 1) * 128], hb[:, j, c * 128:(c + 1) * 128], [hbkey + str(j)], [bk])
            if c % 2 == 0:
                ACT(hT[:, c, :], pmb[:, b, 0:nj * 128], AF.Copy, [bk], [hTkey + str(c)])
            else:
                E("vector", "tensor_copy", [bk], [hTkey + str(c)], out=hT[:, c, :], in_=pmb[:, b, 0:nj * 128])

    def post_norm_res(banks, xap, xkey, gbc, gkey, st, stkey, tmp, tmpkeys, pq):
        c0 = 8 + 3 * pq
        sk = stkey + "q%d" % pq
        for n in range(2):
            b, bk = banks[n]
            ACT(junk[:, 0:512], pm[:, b, :], AF.Square, [bk], [sk + "p%d" % n, "junk"], scale=1.0 / 32.0,
                accum_out=st[:, c0 + n:c0 + n + 1])
        E("vector", "tensor_tensor", [sk + "p0", sk + "p1"], [sk + "r"], out=st[:, c0 + 2:c0 + 3], in0=st[:, c0:c0 + 1],
          in1=st[:, c0 + 1:c0 + 2], op=ALU.add)
        rstd_from_ms(st[:, c0 + 2:c0 + 3], 1, sk + "r")
        for n in range(2):
            b, bk = banks[n]
            E("vector", "scalar_tensor_tensor", [bk, sk + "r", gkey], [tmpkeys[n]],
              out=tmp[:, n * 512:(n + 1) * 512], in0=pm[:, b, :], scalar=st[:, c0 + 2:c0 + 3], in1=gbc[:, n * 512:(n + 1) * 512],
              op0=ALU.mult, op1=ALU.mult)
        E("gpsimd", "tensor_tensor", [xkey, tmpkeys[0], tmpkeys[1]], [xkey], out=xap, in0=xap,
          in1=tmp[:, 0:1024], op=ALU.add)

    for l in range(L):
        xsrc = x_in if l == 0 else y
        arena[0] = PERSIST_END
        win_sb = SB("A_win", [128, 8, INW], BF16)
        wkr_sb = SB("A_wkr", [128, 8, 2 * DK], BF16)
        wuqA = SB("A_wuqA", [128, 3, H * DK], BF16)
        wuqB = SB("A_wuqB", [128, 3, H * DK], BF16)
        wuk_sb = SB("A_wuk", [128, 2, 512], BF16)
        wuv_sb = SB("A_wuv", [128, 2, 512], BF16)
        gpre = SB("A_gpre", [128, D], F32)
        kmT = SB("A_kmT", [128, NSEQ, XH, NMEM], BF16)
        vm = SB("A_vm", [128, NSEQ, 2, 512], BF16)
        A_W_END = arena[0]
        wmkv = SB("A_wmkv", [128, 8, 1024], BF16)
        gmem = SB("A_gmem", [128, D], F32)
        memt = SB("A_memt", [128, 2, D], F32)
        memb = SB("A_memb", [128, 2, D], BF16)
        memT = SB("A_memT", [128, 8, NMEM], BF16)

        DMA("gpsimd", wmkv[:], w_mem_kv[l].rearrange("(k p) f -> p k f", p=128), [], ["wmkv"])
        for (c0, c1, nm) in ((512, 896, "cq"), (896, 1152, "ckv"), (0, 512, "u"), (1184, 1696, "qx")):
            DMA("gpsimd", win_sb[:, :, c0:c1], w_in[l, :, c0:c1].rearrange("(k p) f -> p k f", p=128), [], ["win_" + nm])
        DMA("gpsimd", wkr_sb[:], w_krp[l].rearrange("(k p) f -> p k f", p=128), [], ["wkr"])
        DMA("gpsimd", wuqA[:], w_uq[l].rearrange("(k p) f -> p k f", p=128), [], ["wuqA"])
        DMA("gpsimd", wuqB[:], w_uqB[l].rearrange("(k p) f -> p k f", p=128), [], ["wuqB"])
        DMA("gpsimd", wuk_sb[:], w_uk[l].rearrange("(k p) f -> p k f", p=128), [], ["wuk"])
        DMA("gpsimd", wuv_sb[:], w_uv[l].rearrange("(k p) f -> p k f", p=128), [], ["wuv"])
        DMA("sync", gpre[:], bcv_in[l * 5 + 0].partition_broadcast(128), [], ["gpre"])
        DMA("sync", gmem[:], bcv_in[l * 5 + 4].partition_broadcast(128), [], ["gmem"])
        for si in range(NSEQ):
            DMA("sync", memt[:], mem_in[si * NMEM:(si + 1) * NMEM, :].rearrange("(j p) d -> p j d", p=128), [], ["memt"])
            for j in range(2):
                ACT(junk[:], memt[:, j, :], AF.Square, ["memt"], ["mst", "junk"], scale=1.0 / 32.0, accum_out=small[:, j:j + 1])
            rstd_from_ms(small[:, 0:2], 2, "mst")
            for j in range(2):
                E("vector", "scalar_tensor_tensor", ["memt", "mst", "gmem"], ["memb%d" % j], out=memb[:, j, :],
                  in0=memt[:, j, :], scalar=small[:, j:j + 1], in1=gmem[:], op0=ALU.mult, op1=ALU.mult)
            transposes(memb, "memb", memT, "memT", nj=2)
            mk = ["memT%d" % c for c in range(8)]
            for h in range(XH):
                b, bk = bank()
                for c in range(8):
                    MM(pm[:, b, 0:NMEM], wmkv[:, c, h * 128:(h + 1) * 128], memT[:, c, :], c == 0, c == 7,
                       mk + ["wmkv"], [bk])
                ACT(kmT[:, si, h, :], pm[:, b, 0:NMEM], AF.Copy, [bk], ["kmT%d" % si])
            for j in range(2):
                b, bk = bank()
                for c in range(8):
                    MM(pm[:, b, :], memT[:, c, j * 128:(j + 1) * 128], wmkv[:, c, 512:1024], c == 0, c == 7,
                       mk + ["wmkv"], [bk])
                E("vector", "tensor_copy", [bk], ["vm%d" % si], out=vm[:, si, j, :], in_=pm[:, b, :])
        S.fence()
        arena[0] = A_W_END
        xtA = [SB("A_xt%d" % i, [128, 4, D], F32) for i in range(2)]
        hbA = [SB("A_hb%d" % i, [128, 4, D], BF16) for i in range(2)]
        hTA = [SB("A_hT%d" % i, [128, 8, TT], BF16) for i in range(2)]
        uTA = SB("A_uT", [128, 4, TT], F32)
        sqA = [SB("A_sq%d" % i, [128, TT], F32) for i in range(2)]
        rqA = SB("A_rq", [128, TT], F32)
        rkvA = SB("A_rkv", [128, TT], F32)
        cqn = SB("A_cqn", [128, 3, TT], BF16)
        ckvn = SB("A_ckvn", [128, 2, TT], BF16)
        qxT = SB("A_qxT", [128, 4, TT], BF16)
        cosA = [SB("A_cos%d" % i, [DK, TT], F32) for i in range(2)]
        sinA = [SB("A_sin%d" % i, [DK, TT], F32) for i in range(2)]
        t1A = [SB("A_t1%d" % i, [DK, TT], F32) for i in range(2)]
        t2A = [SB("A_t2%d" % i, [DK, TT], F32) for i in range(2)]
        QTt = SB("A_QTt", [DK, H, TT], BF16)
        KTt = SB("A_KTt", [DK, H, TT], BF16)
        Vt = SB("A_Vt", [128, 4, 512], BF16)
        PmT = [SB("A_PmT%d" % i, [128, 2, TT], BF16) for i in range(2)]
        recA = [SB("A_rec%d" % i, [128, TT], F32) for i in range(2)]
        moT = SB("A_moT", [128, 4, TT], BF16)
        stA = [SB("A_st%d" % i, [128, 16], F32) for i in range(2)]

        def A_loadx(i):
            si, ti, g0, p0, Sl = ALLT[i]
            par = i % 2
            for j in range(4):
                DMA("sync", xtA[par][:, j, :], xsrc[g0 + j * 128:g0 + (j + 1) * 128, :], [], ["Ax%d_%d" % (par, j)])

        def A_s1(i):
            si, ti, g0, p0, Sl = ALLT[i]
            par = i % 2
            xk = ["Ax%d_%d" % (par, j) for j in range(4)]
            DMA("sync", cosA[par][:], cos_in[:, p0:p0 + TT], [], ["Acos%d" % par])
            DMA("sync", sinA[par][:], sin_in[:, p0:p0 + TT], [], ["Asin%d" % par])
            tm_norm(xtA[par], xk, gpre, "gpre", hbA[par], "Ahb%d_" % par, stA[par], "Ast%d" % par)

        def A_s2(i):
            si, ti, g0, p0, Sl = ALLT[i]
            par = i % 2
            transposes(hbA[par], "Ahb%d_" % par, hTA[par], "AhT%d_" % par)
            DMA("sync", HT[:, :, g0:g0 + TT].rearrange("c p t -> p c t"), hTA[par][:],
                ["AhT%d_%d" % (par, c) for c in range(8)], [])

        def A_s3(i):
            si, ti, g0, p0, Sl = ALLT[i]
            par = i % 2
            hk = ["AhT%d_%d" % (par, c) for c in range(8)]
            hT = hTA[par]

            def zchunk(col0, m, wsb=None, wkeys=None, lhs=None):
                b, bk = bank()
                wk = wkeys or ["win_" + ("u" if col0 < 512 else "cq" if col0 < 896 else "ckv" if col0 < 1152 else "qx")]
                for c in range(8):
                    lt = lhs(c) if lhs is not None else win_sb[:, c, col0:col0 + m]
                    MM(pm[0:m, b, :], lt, hT[:, c, :], c == 0, c == 7, hk + wk, [bk])
                return b, bk

            cqb = []
            for c3 in range(3):
                b, bk = zchunk(512 + c3 * 128, 128)
                cqb.append((b, bk))
                ACT(sqA[c3 % 2][:], pm[:, b, :], AF.Square, [bk], ["Asq%d" % (c3 % 2)])
                if c3 == 0:
                    bo_, bok = bank()
                    onesb = (bo_, bok)
                MM(pm[:, onesb[0], :], ones_f[:], sqA[c3 % 2][:], c3 == 0, c3 == 2, ["ones_f", "Asq%d" % (c3 % 2)], [onesb[1]])
            ACT(rqA[:], pm[:, onesb[0], :], AF.Ln, [onesb[1], "epsc"], ["Arq"], scale=1.0 / 384.0, bias=epsc[:, 0:1])
            ACT(rqA[:], rqA[:], AF.Exp, ["Arq"], ["Arq"], scale=-0.5)
            for c3 in range(3):
                b, bk = cqb[c3]
                E("vector", "scalar_tensor_tensor", [bk, "Arq", "vecs"], ["Acqn"], out=cqn[:, c3, :], in0=pm[:, b, :],
                  scalar=vcol(l, c3), in1=rqA[:], op0=ALU.mult, op1=ALU.mult)
            ckb = []
            for c2 in range(2):
                b, bk = zchunk(896 + c2 * 128, 128)
                ckb.append((b, bk))
                ACT(sqA[c2 % 2][:], pm[:, b, :], AF.Square, [bk], ["Asq%d" % (c2 % 2)])
                if c2 == 0:
                    onesb = bank()
                MM(pm[:, onesb[0], :], ones_f[:], sqA[c2 % 2][:], c2 == 0, c2 == 1, ["ones_f", "Asq%d" % (c2 % 2)], [onesb[1]])
            ACT(rkvA[:], pm[:, onesb[0], :], AF.Ln, [onesb[1], "epsc"], ["Arkv"], scale=1.0 / 256.0, bias=epsc[:, 0:1])
            ACT(rkvA[:], rkvA[:], AF.Exp, ["Arkv"], ["Arkv"], scale=-0.5)
            for c2 in range(2):
                b, bk = ckb[c2]
                E("vector", "scalar_tensor_tensor", [bk, "Arkv", "vecs"], ["Ackvn"], out=ckvn[:, c2, :], in0=pm[:, b, :],
                  scalar=vcol(l, 3 + c2), in1=rkvA[:], op0=ALU.mult, op1=ALU.mult)
            bA, bAk = zchunk(0, DK, wkeys=["wkr"], lhs=lambda c: wkr_sb[:, c, 0:DK])
            bB, bBk = zchunk(0, DK, wkeys=["wkr"], lhs=lambda c: wkr_sb[:, c, DK:2 * DK])
            E("vector", "tensor_tensor", [bAk, "Acos%d" % par], ["At1_%d" % par], out=t1A[par][64:96, :], in0=pm[64:96, bA, :],
              in1=cosA[par][64:96, :], op=ALU.mult)
            E("vector", "tensor_tensor", [bBk, "Asin%d" % par], ["At2_%d" % par], out=t2A[par][64:96, :], in0=pm[64:96, bB, :],
              in1=sinA[par][64:96, :], op=ALU.mult)
            E("gpsimd", "tensor_tensor", ["At1_%d" % par, "At2_%d" % par], ["AKr0"], out=KTt[64:96, 0, :],
              in0=t1A[par][64:96, :], in1=t2A[par][64:96, :], op=ALU.add)
            for g in range(4):
                b, bk = zchunk(g * 128, 128)
                if g % 2 == 0:
                    ACT(uTA[:, g, :], pm[:, b, :], AF.Copy, [bk], ["AuT%d" % g])
                else:
                    E("vector", "tensor_copy", [bk], ["AuT%d" % g], out=uTA[:, g, :], in_=pm[:, b, :])
            DMA("sync", U[:, :, g0:g0 + TT].rearrange("g p t -> p g t"), uTA[:], ["AuT%d" % g for g in range(4)], [])
            for hx in range(4):
                b, bk = zchunk(1184 + hx * 128, 128)
                if hx % 2 == 0:
                    E("vector", "tensor_copy", [bk], ["Aqx%d" % hx], out=qxT[:, hx, :], in_=pm[:, b, :])
                else:
                    ACT(qxT[:, hx, :], pm[:, b, :], AF.Copy, [bk], ["Aqx%d" % hx])

        def A_s4(i):
            si, ti, g0, p0, Sl = ALLT[i]
            par = i % 2
            for h in range(H):
                bA, bAk = bank()
                for c in range(3):
                    MM(pm[0:DK, bA, :], wuqA[:, c, h * DK:(h + 1) * DK], cqn[:, c, :], c == 0, c == 2, ["wuqA", "Acqn"], [bAk])
                bB, bBk = bank()
                for c in range(3):
                    MM(pm[0:DK, bB, :], wuqB[:, c, h * DK:(h + 1) * DK], cqn[:, c, :], c == 0, c == 2, ["wuqB", "Acqn"], [bBk])
                tp = h % 2
                E("vector", "tensor_tensor", [bAk, "Acos%d" % par], ["At1_%d" % tp], out=t1A[tp][:, :], in0=pm[0:DK, bA, :],
                  in1=cosA[par][:, :], op=ALU.mult)
                E("vector", "tensor_tensor", [bBk, "Asin%d" % par], ["At2_%d" % tp], out=t2A[tp][:, :], in0=pm[0:DK, bB, :],
                  in1=sinA[par][:, :], op=ALU.mult)
                E("gpsimd", "tensor_tensor", ["At1_%d" % tp, "At2_%d" % tp], ["AQ%d" % h], out=QTt[:, h, :], in0=t1A[tp][:, :],
                  in1=t2A[tp][:, :], op=ALU.add)
            DMA("sync", QT[:, :, g0:g0 + TT].rearrange("h p t -> p h t"), QTt[:], ["AQ%d" % h for h in range(H)], [])
            for h in range(H):
                b, bk = bank()
                for c in range(2):
                    MM(pm[0:64, b, :], wuk_sb[:, c, h * 64:(h + 1) * 64], ckvn[:, c, :], c == 0, c == 1, ["wuk", "Ackvn"], [bk])
                ACT(KTt[0:64, h, :], pm[0:64, b, :], AF.Copy, [bk], ["AKn%d" % h])
            DMA("sync", KT[:, 0:64, g0:g0 + TT].rearrange("h p t -> p h t"), KTt[0:64, :, :],
                ["AKn%d" % h for h in range(H)], [])
            DMA("sync", KT[:, 64:96, g0:g0 + TT].rearrange("h p t -> p h t"),
                KTt[64:96, 0:1, :].to_broadcast([32, H, TT]), ["AKr0"], [])
            for j in range(4):
                b, bk = bank()
                for c in range(2):
                    MM(pm[:, b, :], ckvn[:, c, j * 128:(j + 1) * 128], wuv_sb[:, c, :], c == 0, c == 1, ["wuv", "Ackvn"], [bk])
                if j % 2 == 0:
                    E("vector", "tensor_copy", [bk], ["AV%d" % j], out=Vt[:, j, :], in_=pm[:, b, :])
                else:
                    ACT(Vt[:, j, :], pm[:, b, :], AF.Copy, [bk], ["AV%d" % j])
            DMA("sync", V[g0:g0 + TT, :].rearrange("(j p) f -> p j f", p=128), Vt[:], ["AV%d" % j for j in range(4)], [])
            def mQK(hx):
                pp = hx % 2
                for mt in range(2):
                    b, bk = bank()
                    MM(pm[:, b, :], kmT[:, si, hx, mt * 128:(mt + 1) * 128], qxT[:, hx, :], True, True,
                       ["kmT%d" % si, "Aqx%d" % hx], [bk])
                    ACT(PmT[pp][:, mt, :], pm[:, b, :], AF.Exp, [bk], ["APm%d_%d" % (pp, mt)], scale=scale_x)

            def mPV(hx):
                pp = hx % 2
                bo_, bok = bank()
                bd_, bdk = bank()
                for mt in range(2):
                    MM(pm[:, bo_, :], vm[:, si, mt, hx * 128:(hx + 1) * 128], PmT[pp][:, mt, :], mt == 0, mt == 1,
                       ["vm%d" % si, "APm%d_%d" % (pp, mt)], [bok])
                for mt in range(2):
                    MM(pm[:, bd_, :], ones_bf[:], PmT[pp][:, mt, :], mt == 0, mt == 1, ["ones_bf", "APm%d_%d" % (pp, mt)], [bdk])
                ACT(recA[pp][:], pm[:, bd_, :], AF.Ln, [bdk], ["Arec%d" % pp])
                ACT(recA[pp][:], recA[pp][:], AF.Exp, ["Arec%d" % pp], ["Arec%d" % pp], scale=-1.0)
                E("vector", "tensor_tensor", [bok, "Arec%d" % pp], ["Amo%d" % hx], out=moT[:, hx, :], in0=pm[:, bo_, :],
                  in1=recA[pp][:], op=ALU.mult)

            mQK(0)
            for hx in range(XH):
                if hx + 1 < XH:
                    mQK(hx + 1)
                mPV(hx)
            DMA("sync", MO[:, :, g0:g0 + TT].rearrange("g p t -> p g t"), moT[:], ["Amo%d" % hx for hx in range(XH)], [])

        NT = len(ALLT)
        A_loadx(0)
        if NT > 1:
            A_loadx(1)
        A_s1(0)
        A_s2(0)
        for i in range(NT):
            if i + 2 < NT:
                A_loadx(i + 2)
            if i + 1 < NT:
                A_s1(i + 1)
            A_s3(i)
            if i + 1 < NT:
                A_s2(i + 1)
            A_s4(i)
        S.fence()

        arena[0] = PERSIST_END
        wg_sb = SB("C_wg", [128, 8, 3 * D], BF16)
        wbr_sb = SB("C_wbr", [128, 12, D], BF16)
        wo_sb = SB("C_wo", [128, 8, D], BF16)
        mix_sb = SB("C_mix", [128, 4, 128], BF16)
        gpost = SB("C_gpost", [128, D], F32)
        KTMAX = SMAX // 128
        KTs = [SB("B_KTs%d" % i, [DK, SMAX], BF16) for i in range(2)]
        QTs = [SB("B_QTs%d" % i, [DK, SMAX], BF16) for i in range(2)]
        Vsb = [SB("B_Vsb%d" % i, [128, KTMAX, 128], BF16) for i in range(2)]
        Pt = [SB("B_Pt%d" % i, [128, 2, TT], BF16) for i in range(3)]
        recB = [SB("B_rec%d" % i, [64, TT], F32) for i in range(2)]
        boB = [SB("B_bo%d" % i, [64, TT], BF16) for i in range(2)]
        for i in range(2):
            E("gpsimd", "memset", [], ["BVones%d" % i], Vsb[i][:, :, 64:128], 1.0)
        for c in range(8):
            DMA("gpsimd", wg_sb[:, c, :], w_gate[l, c * 128:(c + 1) * 128, :], [], ["wg%d" % c])
        for n in range(3):
            DMA("gpsimd", wbr_sb[:, n * 4:(n + 1) * 4, :], w_branch[l, n].rearrange("(k p) f -> p k f", p=128), [], ["wbr%d" % n])
        DMA("gpsimd", wo_sb[:], w_out[l].rearrange("(k p) f -> p k f", p=128), [], ["wo"])
        DMA("gpsimd", mix_sb[:], pool_mix[l].rearrange("g c d -> c g d"), [], ["mix"])
        DMA("gpsimd", gpost[:], bcv_in[l * 5 + 1].partition_broadcast(128), [], ["gpost"])
        units = [(si, h) for si in range(NSEQ) for h in range(H)]

        def B_load(u):
            si, h = units[u]
            par = u % 2
            Sl = seq_lens[si]
            o = seq_off[si]
            DMA("sync", KTs[par][:, 0:Sl], KT[h, :, o:o + Sl], [], ["BK%d" % par])
            DMA("sync", QTs[par][:, 0:Sl], QT[h, :, o:o + Sl], [], ["BQ%d" % par])
            nkt = Sl // 128
            for k0 in range(0, nkt, 16):
                k1 = min(nkt, k0 + 16)
                DMA("sync", Vsb[par][:, k0:k1, 0:64],
                    V[o + k0 * 128:o + k1 * 128, h * 64:(h + 1) * 64].rearrange("(k p) f -> p k f", p=128),
                    [], ["BV%d_%d" % (par, k0)])

        sc_banks = [(0, 1), (2, 3), (4, 5)]
        flat = []
        qtc = 0
        for u, (si, h) in enumerate(units):
            Sl = seq_lens[si]
            nkt = Sl // 128
            ng = nkt // 2
            for qt in range(Sl // TT):
                for g in range(ng):
                    flat.append((u, qt, g, ng, nkt, qtc))
                qtc += 1
        NG = len(flat)

        def B_QK(k):
            u, qt, g, ng, nkt, qc = flat[k]
            par = u % 2
            s_ = k % 3
            q0 = qt * TT
            for t in range(2):
                b = sc_banks[s_][t]
                kt = 2 * g + t
                MM(pm[:, b, :], KTs[par][:, kt * 128:(kt + 1) * 128], QTs[par][:, q0:q0 + TT], True, True,
                   ["BK%d" % par, "BQ%d" % par], ["ps%d" % b])

        def B_EXP(k):
            s_ = k % 3
            b0 = sc_banks[s_][0]
            ACT(Pt[s_][:, :, :], pm[:, b0:b0 + 2, :], AF.Exp, ["ps%d" % b0, "ps%d" % (b0 + 1)], ["BPt%d" % s_],
                scale=scale_mla)

        def B_PV(k):
            u, qt, g, ng, nkt, qc = flat[k]
            si, h = units[u]
            par = u % 2
            s_ = k % 3
            ob = 6 + (qc % 2)
            obk = "ps%d" % ob
            kvkeys = ["BV%d_%d" % (par, k0) for k0 in range(0, nkt, 16)] + ["BVones%d" % par]
            for t in range(2):
                kt = 2 * g + t
                MM(pm[:, ob, :], Vsb[par][:, kt, :], Pt[s_][:, t, :], kt == 0, kt == nkt - 1, kvkeys + ["BPt%d" % s_], [obk])
            if g == ng - 1:
                op_ = qc % 2
                o = seq_off[si]
                q0 = qt * TT
                E("vector", "reciprocal", [obk], ["Brec%d" % op_], out=recB[op_][:], in_=pm[64:128, ob, :])
                E("vector", "tensor_tensor", [obk, "Brec%d" % op_], ["Bbo%d" % op_], out=boB[op_][:], in0=pm[0:64, ob, :],
                  in1=recB[op_][:], op=ALU.mult)
                c4 = h // 2
                r0 = (h % 2) * 64
                DMA("sync", BO[c4, r0:r0 + 64, o + q0:o + q0 + TT], boB[op_][:], ["Bbo%d" % op_], [])

        B_load(0)
        if len(units) > 1:
            B_load(1)
        for k in range(min(2, NG)):
            B_QK(k)
            B_EXP(k)
        for k in range(NG):
            if k + 2 < NG:
                B_QK(k + 2)
                B_EXP(k + 2)
            B_PV(k)
            if k + 1 < NG and flat[k + 1][0] != flat[k][0]:
                if flat[k][0] + 2 < len(units):
                    B_load(flat[k][0] + 2)
        S.fence()

        arena[0] = PERSIST_END
        wg_sb = SB("C_wg", [128, 8, 3 * D], BF16)
        wbr_sb = SB("C_wbr", [128, 12, D], BF16)
        wo_sb = SB("C_wo", [128, 8, D], BF16)
        mix_sb = SB("C_mix", [128, 4, 128], BF16)
        gpost = SB("C_gpost", [128, D], F32)
        hTC = [SB("C_hT%d" % i, [128, 8, TT], BF16) for i in range(2)]
        boC = [SB("C_bo%d" % i, [128, 4, TT], BF16) for i in range(2)]
        moC = [SB("C_mo%d" % i, [128, 4, TT], BF16) for i in range(2)]
        uh = [SB("C_uh%d" % i, [128, 4, TT + 16], F32) for i in range(2)]
        pa = SB("C_pa", [128, TT + 16], F32)
        pb = SB("C_pb", [128, TT + 16], F32)
        dT = SB("C_dT", [128, 4, TT], BF16)
        aT = SB("C_aT", [128, 4, TT], BF16)
        gs = [SB("C_gs%d" % i, [128, TT], F32) for i in range(3)]
        mt_ = [SB("C_m%d" % i, [128, TT], F32) for i in range(3)]
        mT = SB("C_mT", [128, 8, TT], BF16)
        xtC = SB("C_xt", [128, 4, D], F32)
        tmpC = [SB("C_tmp%d" % i, [128, D], F32) for i in range(2)]
        stC = SB("C_st", [128, 16], F32)
        fxC = SB("C_fx", [128, 16], F32)


        def C_load(i):
            si, ti, g0, p0, Sl = ALLT[i]
            par = i % 2
            DMA("sync", hTC[par][:], HT[:, :, g0:g0 + TT].rearrange("c p t -> p c t"), [], ["ChT%d" % par])
            DMA("sync", boC[par][:], BO[:, :, g0:g0 + TT].rearrange("c p t -> p c t"), [], ["Cbo%d" % par])
            DMA("sync", moC[par][:], MO[:, :, g0:g0 + TT].rearrange("c p t -> p c t"), [], ["Cmo%d" % par])
            lo = 8 if p0 == 0 else 0
            hi = TT + 8 if p0 + TT == Sl else TT + 16
            if lo > 0:
                E("gpsimd", "memset", [], ["Cuh%d" % par], uh[par][:, :, 0:8], 0.0)
            if hi < TT + 16:
                E("gpsimd", "memset", [], ["Cuh%d" % par], uh[par][:, :, TT + 8:TT + 16], 0.0)
            DMA("sync", uh[par][:, :, lo:hi], U[:, :, g0 - 8 + lo:g0 - 8 + hi].rearrange("g p t -> p g t"), [], ["Cuh%d" % par])

        def C_pool_g(i, g):
            si, ti, g0, p0, Sl = ALLT[i]
            par = i % 2
            uk = "Cuh%d" % par
            w = 2 << g
            lo2, hi2 = [(8, 520), (7, 521), (5, 523), (1, 527)][g]
            E("gpsimd", "tensor_tensor", [uk], ["Cpa"], out=pa[:, lo2:hi2], in0=uh[par][:, g, lo2 - 1:hi2 - 1],
              in1=uh[par][:, g, lo2:hi2], op=ALU.add)
            cur, curk, oth, othk = pa, "Cpa", pb, "Cpb"
            if g >= 1:
                lo4, hi4 = [(8, 520), (6, 522), (2, 526)][g - 1]
                E("gpsimd", "tensor_tensor", [curk], [othk], out=oth[:, lo4:hi4], in0=cur[:, lo4 - 1:hi4 - 1],
                  in1=cur[:, lo4 + 1:hi4 + 1], op=ALU.add)
                cur, curk, oth, othk = oth, othk, cur, curk
            if g >= 2:
                lo8, hi8 = [(8, 520), (4, 524)][g - 2]
                E("gpsimd", "tensor_tensor", [curk], [othk], out=oth[:, lo8:hi8], in0=cur[:, lo8 - 2:hi8 - 2],
                  in1=cur[:, lo8 + 2:hi8 + 2], op=ALU.add)
                cur, curk, oth, othk = oth, othk, cur, curk
            if g >= 3:
                E("gpsimd", "tensor_tensor", [curk], [othk], out=oth[:, 8:520], in0=cur[:, 4:516],
                  in1=cur[:, 12:524], op=ALU.add)
                cur, curk, oth, othk = oth, othk, cur, curk
            if p0 == 0:
                E("gpsimd", "tensor_tensor", [curk, "pfix"], ["Cfx0"], out=fxC[:, 0:8], in0=cur[:, 8:16],
                  in1=pfix[:, 0, g, :], op=ALU.mult)
            if p0 + TT == Sl:
                E("gpsimd", "tensor_tensor", [curk, "pfix"], ["Cfx1"], out=fxC[:, 8:16], in0=cur[:, 512:520],
                  in1=pfix[:, 1, g, :], op=ALU.mult)
            E("gpsimd", "tensor_tensor", [curk, "pfix"], [curk], out=cur[:, 8:520], in0=cur[:, 8:520],
              in1=pfix[:, 1, g, 0:1].to_broadcast([128, TT]), op=ALU.mult)
            if p0 == 0:
                E("gpsimd", "tensor_copy", ["Cfx0", curk], [curk], out=cur[:, 8:16], in_=fxC[:, 0:8])
            if p0 + TT == Sl:
                E("gpsimd", "tensor_copy", ["Cfx1", curk], [curk], out=cur[:, 512:520], in_=fxC[:, 8:16])
            E("gpsimd", "tensor_tensor", [curk, uk], ["CdT%d" % g], out=dT[:, g, :], in0=cur[:, 8:520],
              in1=uh[par][:, g, 8:520], op=ALU.subtract)

        def C_poolmix(i):
            for g in range(4):
                b, bk = bank()
                MM(pm[:, b, :], mix_sb[:, g, :], dT[:, g, :], True, True, ["mix", "CdT%d" % g], [bk])
                ACT(aT[:, g, :], pm[:, b, :], AF.Copy, [bk, "vecs"], ["CaT%d" % g], scale=vcol(l, 5 + g))

        def C_main(i):
            si, ti, g0, p0, Sl = ALLT[i]
            par = i % 2
            for j in range(4):
                DMA("sync", xtC[:, j, :], xsrc[g0 + j * 128:g0 + (j + 1) * 128, :], [], ["Cx%d" % j])
            srcs = [(aT, ["CaT%d" % g for g in range(4)]), (boC[par], ["Cbo%d" % par]), (moC[par], ["Cmo%d" % par])]
            for jc in range(8):
                gb = []
                for n in range(3):
                    b, bk = bank()
                    for c in range(8):
                        MM(pm[:, b, :], wg_sb[:, c, n * D + jc * 128:n * D + (jc + 1) * 128], hTC[par][:, c, :], c == 0, c == 7,
                           ["wg%d" % c, "ChT%d" % par], [bk])
                    ACT(gs[n][:], pm[:, b, :], AF.Sigmoid, [bk, "vecs"], ["Cgs%d" % n], bias=vcol(l, 9 + n * 8 + jc))
                for n in range(3):
                    b, bk = bank()
                    src, sk = srcs[n]
                    for c in range(4):
                        MM(pm[:, b, :], wbr_sb[:, n * 4 + c, jc * 128:(jc + 1) * 128], src[:, c, :], c == 0, c == 3,
                           ["wbr%d" % n] + sk, [bk])
                    E("vector", "tensor_tensor", [bk, "Cgs%d" % n], ["Cm%d" % n], out=mt_[n][:], in0=pm[:, b, :], in1=gs[n][:],
                      op=ALU.mult)
                E("vector", "tensor_tensor", ["Cm0", "Cm1"], ["Cm0"], out=mt_[0][:], in0=mt_[0][:], in1=mt_[1][:], op=ALU.add)
                E("vector", "tensor_tensor", ["Cm0", "Cm2"], ["CmT%d" % jc], out=mT[:, jc, :], in0=mt_[0][:], in1=mt_[2][:],
                  op=ALU.add)
                if jc < 4 and i + 1 < NT:
                    C_pool_g(i + 1, 3 - jc)
            if i + 1 < NT:
                C_poolmix(i + 1)
            mk = ["CmT%d" % jc for jc in range(8)]
            for j in range(4):
                banks = []
                for n in range(2):
                    b, bk = bank()
                    banks.append((b, bk))
                    for c in range(8):
                        MM(pm[:, b, :], mT[:, c, j * 128:(j + 1) * 128], wo_sb[:, c, n * 512:(n + 1) * 512], c == 0, c == 7,
                           mk + ["wo"], [bk])
                post_norm_res(banks, xtC[:, j, :], "Cx%d" % j, gpost, "gpost", stC, "Cst", tmpC[j % 2], ["Ctmp%d_0" % (j % 2), "Ctmp%d_1" % (j % 2)], j % 2)
                DMA("sync", XM[g0 + j * 128:g0 + (j + 1) * 128, :], xtC[:, j, :], ["Cx%d" % j], [])

        C_load(0)
        for g in range(4):
            C_pool_g(0, g)
        C_poolmix(0)
        for i in range(NT):
            if i + 1 < NT:
                C_load(i + 1)
            C_main(i)
        S.fence()

        arena[0] = PERSIST_END
        wgu_sb = SB("D_wgu", [128, 8, 2 * DFF], BF16)
        wdn_sb = SB("D_wdn", [128, NFF, D], BF16)
        gfpre = SB("D_gfpre", [128, D], F32)
        gfpost = SB("D_gfpost", [128, D], F32)
        xinD = [SB("D_xin%d" % i, [128, D], F32) for i in range(2)]
        hbD = [SB("D_hb%d" % i, [128, D], BF16) for i in range(2)]
        hTD = SB("D_hT", [128, 8, TT], BF16)
        actT = SB("D_actT", [128, NFF, TT], BF16)
        sgD2 = SB("D_sg2", [128, 2, TT], F32)
        sgD = [sgD2[:, 0, :], sgD2[:, 1, :]]
        tmpD = SB("D_tmp", [128, D], F32)
        xrD = [SB("D_xr%d" % i, [128, D], F32) for i in range(2)]
        stD = SB("D_st", [128, 16], F32)
        DMA("sync", gfpre[:], bcv_in[l * 5 + 2].partition_broadcast(128), [], ["gfpre"])
        DMA("sync", gfpost[:], bcv_in[l * 5 + 3].partition_broadcast(128), [], ["gfpost"])

        def D_norm(i, j):
            g0 = ALLT[i][2]
            q = j % 2
            DMA("sync", xinD[q][:], XM[g0 + j * 128:g0 + (j + 1) * 128, :], [], ["Dxin%d" % q])
            ACT(junk[:], xinD[q][:], AF.Square, ["Dxin%d" % q], ["Dst%d" % j, "junk"], scale=1.0 / 32.0, accum_out=stD[:, j:j + 1])
            rstd_from_ms(stD[:, j:j + 1], 1, "Dst%d" % j)
            E("vector", "scalar_tensor_tensor", ["Dxin%d" % q, "Dst%d" % j, "gfpre"], ["Dhb%d" % q], out=hbD[q][:],
              in0=xinD[q][:], scalar=stD[:, j:j + 1], in1=gfpre[:], op0=ALU.mult, op1=ALU.mult)

        def D_tr(i, j):
            q = j % 2
            b, bk = bank()
            for c in range(8):
                TR(pmb[:, b, c * 128:(c + 1) * 128], hbD[q][:, c * 128:(c + 1) * 128], ["Dhb%d" % q], [bk])
            src = pmb[:, b, 0:1024].rearrange("p (c t) -> p c t", c=8)
            if j % 2 == 0:
                ACT(hTD[:, :, j * 128:(j + 1) * 128], src, AF.Copy, [bk], ["DhT_%d" % j])
            else:
                E("vector", "tensor_copy", [bk], ["DhT_%d" % j], out=hTD[:, :, j * 128:(j + 1) * 128], in_=src)

        def D_s3(i):
            hk = ["DhT_%d" % j for j in range(4)]
            for f in range(NFF):
                bg, bgk = bank()
                for c in range(8):
                    MM(pm[:, bg, :], wgu_sb[:, c, f * 128:(f + 1) * 128], hTD[:, c, :], c == 0, c == 7, hk + ["wgu_g%d" % (f // 2)], [bgk])
                bu, buk = bank()
                for c in range(8):
                    MM(pm[:, bu, :], wgu_sb[:, c, DFF + f * 128:DFF + (f + 1) * 128], hTD[:, c, :], c == 0, c == 7,
                       hk + ["wgu_u%d" % (f // 2)], [buk])
                ACT(sgD[f % 2], pm[:, bg, :], AF.Silu, [bgk], ["Dsg%d" % (f % 2)])
                E("vector", "tensor_tensor", [buk, "Dsg%d" % (f % 2)], ["Dact%d" % f], out=actT[:, f, :], in0=pm[:, bu, :],
                  in1=sgD[f % 2], op=ALU.mult)

        def D_down(i, j):
            g0 = ALLT[i][2]
            q = j % 2
            ak = ["Dact%d" % f for f in range(NFF)]
            DMA("sync", xrD[q][:], XM[g0 + j * 128:g0 + (j + 1) * 128, :], [], ["Dxr%d" % q])
            banks = []
            for n in range(2):
                b, bk = bank()
                banks.append((b, bk))
                for f in range(NFF):
                    MM(pm[:, b, :], actT[:, f, j * 128:(j + 1) * 128], wdn_sb[:, f, n * 512:(n + 1) * 512], f == 0,
                       f == NFF - 1, ak + ["wdn%d" % f], [bk])
            if q == 0:
                post_norm_res(banks, xrD[q][:], "Dxr%d" % q, gfpost, "gfpost", stD, "Dst2", tmpD, ["Dtmp0", "Dtmp1"], 0)
            else:
                post_norm_res(banks, xrD[q][:], "Dxr%d" % q, gfpost, "gfpost", stD, "Dst2", sgD2[:].rearrange("p a t -> p (a t)"),
                              ["Dsg0", "Dsg1"], 1)
            DMA("sync", y[g0 + j * 128:g0 + (j + 1) * 128, :], xrD[q][:], ["Dxr%d" % q], [])

        for j in range(4):
            D_norm(0, j)
            D_tr(0, j)
        for fb in range(NFF // 2):
            for half, nm in ((0, "g"), (1, "u")):
                c0 = half * DFF + fb * 256
                DMA("gpsimd", wgu_sb[:, :, c0:c0 + 256], w_gu[l, :, c0:c0 + 256].rearrange("(k p) f -> p k f", p=128), [],
                    ["wgu_%s%d" % (nm, fb)])
        for f0 in range(0, NFF, 2):
            DMA("gpsimd", wdn_sb[:, f0:f0 + 2, :], w_down[l, f0 * 128:(f0 + 2) * 128, :].rearrange("(k p) f -> p k f", p=128),
                [], ["wdn%d" % f0, "wdn%d" % (f0 + 1)])
        for i in range(NT):
            D_s3(i)
            nx = i + 1 < NT
            if nx:
                D_norm(i + 1, 0)
                D_norm(i + 1, 1)
            D_down(i, 0)
            D_down(i, 1)
            if nx:
                D_tr(i + 1, 0)
                D_tr(i + 1, 1)
                D_norm(i + 1, 2)
                D_norm(i + 1, 3)
            D_down(i, 2)
            if nx:
                D_tr(i + 1, 2)
                D_tr(i + 1, 3)
            D_down(i, 3)
        S.fence()

    S.emit()
    return nc, S


def host_consts(smax):
    inv = 1.0 / (10000.0 ** (np.arange(0, 32, 2, dtype=np.float32) / 32))
    ang = np.arange(smax, dtype=np.float32)[:, None] * inv[None, :]
    cos = np.cos(ang).astype(np.float32).T
    sin = np.sin(ang).astype(np.float32).T
    cos_t = np.ones((DK, smax), np.float32)
    sin_t = np.zeros((DK, smax), np.float32)
    cos_t[64:80] = cos
    cos_t[80:96] = cos
    sin_t[64:80] = -sin
    sin_t[80:96] = sin
    pf = np.zeros((2, 4, 8), np.float32)
    for g in range(4):
        w = 2 << g
        half = w // 2
        for t in range(8):
            pf[0, g, t] = 1.0 / (t + half) if t < half else 1.0 / w
            fe = 8 - t
            pf[1, g, t] = 1.0 / (fe + half) if fe < half else 1.0 / w
    pfix = np.broadcast_to(pf.reshape(1, -1), (128, 64)).copy()
    return cos_t, sin_t, np.eye(128, dtype=np.float32), pfix


def host_weights(w, L):
    f = lambda a: np.ascontiguousarray(np.asarray(a, dtype=np.float32))
    w_in = f(w["w_in"])
    w_krp = np.zeros((L, D, 2, DK), np.float32)
    w_krp[:, :, 0, 64:96] = w_in[:, :, 1152:1184]
    w_krp[:, :, 1, 64:80] = w_in[:, :, 1168:1184]
    w_krp[:, :, 1, 80:96] = w_in[:, :, 1152:1168]
    w_uq = f(w["w_uq"])
    wq4 = w_uq.reshape(L, 384, H, DK)
    w_uqB = np.zeros((L, 384, H, DK), np.float32)
    w_uqB[:, :, :, 64:80] = wq4[:, :, :, 80:96]
    w_uqB[:, :, :, 80:96] = wq4[:, :, :, 64:80]
    NV = 33
    vecs = np.zeros((128, L, NV), np.float32)
    vecs[:, :, 0:3] = f(w["q_norm"]).reshape(L, 3, 128).transpose(2, 0, 1)
    vecs[:, :, 3:5] = f(w["kv_norm"]).reshape(L, 2, 128).transpose(2, 0, 1)
    vecs[:, :, 5:9] = f(w["pool_scale"]).reshape(L, 4, 128).transpose(2, 0, 1)
    vecs[:, :, 9:33] = f(w["b_gate"]).reshape(L, 24, 128).transpose(2, 0, 1)
    bcv = np.stack([f(w["ln_mix_pre"]), f(w["ln_mix_post"]), f(w["ln_ffn_pre"]), f(w["ln_ffn_post"]), f(w["mem_norm"])],
                   axis=1).reshape(L * 5, D)
    return {
        "w_in": w_in, "w_krp": w_krp.reshape(L, D, 2 * DK), "w_uq": w_uq, "w_uqB": w_uqB.reshape(L, 384, H * DK),
        "w_uk": f(w["w_uk"]), "w_uv": f(w["w_uv"]), "pool_mix": f(w["pool_mix"]), "w_mem_kv": f(w["w_mem_kv"]),
        "w_branch": f(w["w_branch"]), "w_gate": f(w["w_gate"]), "w_out": f(w["w_out"]), "w_gu": f(w["w_gu"]),
        "w_down": f(w["w_down"]), "vecs": np.ascontiguousarray(vecs.reshape(128, L * NV)), "bcv": np.ascontiguousarray(bcv),
    }


_NC_CACHE = {}


def run_cores(xs, mems, weights, seq_lens, depth, debug=False):
    key = (tuple(seq_lens), depth, debug)
    if key not in _NC_CACHE:
        _NC_CACHE[key] = build_nc(seq_lens, depth, debug)
    nc, _ = _NC_CACHE[key]
    hw = host_weights(weights, depth)
    cos_t, sin_t, ident, pfix = host_consts(max(seq_lens))
    base = dict(hw)
    base.update({"cos_t": cos_t, "sin_t": sin_t, "ident": ident, "pfix": pfix})
    in_maps = []
    for xc, mc in zip(xs, mems):
        m = dict(base)
        m["x"] = np.ascontiguousarray(xc, dtype=np.float32)
        m["mem"] = np.ascontiguousarray(mc, dtype=np.float32)
        in_maps.append(m)
    res = run_bass_kernel_spmd(nc, in_maps, core_ids=list(range(len(xs))))
    return res.results


def kernel(x_prompt, x_sample, mem_prompt, mem_sample, **weights):
    x_prompt = np.asarray(x_prompt)
    x_sample = np.asarray(x_sample)
    mem_prompt = np.asarray(mem_prompt)
    mem_sample = np.asarray(mem_sample)
    B, SP, _ = x_prompt.shape
    B2, SS, _ = x_sample.shape
    n = 8
    seq_lens = (SP, SS, SS)
    depth = int(np.asarray(weights["w_in"]).shape[0])
    xs, mems = [], []
    for c in range(n):
        xs.append(np.concatenate([x_prompt[c], x_sample[2 * c], x_sample[2 * c + 1]], axis=0))
        mems.append(np.concatenate([mem_prompt[c], mem_sample[2 * c], mem_sample[2 * c + 1]], axis=0))
    results = run_cores(xs, mems, weights, seq_lens, depth)
    y_prompt = np.empty((B, SP, D), np.float32)
    y_sample = np.empty((B2, SS, D), np.float32)
    for c in range(n):
        yc = results[c]["y"]
        y_prompt[c] = yc[0:SP]
        y_sample[2 * c] = yc[SP:SP + SS]
        y_sample[2 * c + 1] = yc[SP + SS:SP + 2 * SS]
    return (y_prompt, y_sample)
```
